# Optimizing a Trainium2 kernel written in Bass

```python
import jax, jax.numpy as jnp
from jax import lax
import numpy as np

D_MODEL = 1024
BATCH = 4
SEQ = 4096
DEPTH = 4

CHUNK = 64
GMLP_BLOCK = 128
A_HEADS = 4
A_HEAD_DIM = 128
D_A = A_HEADS * A_HEAD_DIM
B_GROUPS = 8
D_B = 512
B_CONV = 3
C_WINDOWS = (2, 4, 8, 16)
C_GROUPS = len(C_WINDOWS)
D_C = D_MODEL // C_GROUPS
D_FF = 2816
FFN_CONV = 3
N_EVEN = (DEPTH + 1) // 2
N_ODD = DEPTH // 2
D_IN_EVEN = 2 * D_A + 3 * D_B
ALPHA = (2.0 * DEPTH) ** 0.25
BETA = (8.0 * DEPTH) ** -0.25
LN_EPS = 1e-5

kernel_name = "hybrid_gmlp_shortconv_pool_convffn_deepnorm"


def layer_norm(x, g, b):
    xf = x.astype(jnp.float32)
    mu = jnp.mean(xf, axis=-1, keepdims=True)
    var = jnp.mean(jnp.square(xf - mu), axis=-1, keepdims=True)
    y = (xf - mu) * lax.rsqrt(var + LN_EPS)
    return (y * g.astype(jnp.float32) + b.astype(jnp.float32)).astype(x.dtype)


def causal_dwconv(h, w):
    K = w.shape[0]
    S = h.shape[1]
    hp = jnp.pad(h, ((0, 0), (K - 1, 0), (0, 0)))
    y = w[0] * hp[:, 0:S]
    for k in range(1, K):
        y = y + w[k] * hp[:, k:k + S]
    return y


def gmlp_chunk_mask():
    c = jnp.arange(GMLP_BLOCK) // CHUNK
    return c[None, :] <= c[:, None]


def gmlp_mixer(uv, ws, bs, ln_g, ln_b):
    Bn, S, _ = uv.shape
    u, v = jnp.split(jax.nn.gelu(uv), 2, axis=-1)
    v = v.reshape(Bn, S, A_HEADS, A_HEAD_DIM)
    v = layer_norm(v, ln_g.reshape(A_HEADS, A_HEAD_DIM), ln_b.reshape(A_HEADS, A_HEAD_DIM))
    v = v.reshape(Bn, S // GMLP_BLOCK, GMLP_BLOCK, A_HEADS, A_HEAD_DIM)
    w = jnp.where(gmlp_chunk_mask()[None], ws, jnp.zeros((), ws.dtype))
    s = jnp.einsum('hij,bnjhd->bnihd', w, v) + bs.T[None, None, :, :, None]
    return u * s.reshape(Bn, S, D_A)


def shortconv_mixer(bch, conv_w):
    gb, gc, h = jnp.split(bch, 3, axis=-1)
    return gb * causal_dwconv(gc * h, conv_w)


def pool_mixer(x, w_c, scale):
    Bn, S, _ = x.shape
    xf = x.astype(jnp.float32)
    cs = jnp.cumsum(xf, axis=1)
    t = jnp.arange(1, S + 1, dtype=jnp.float32)[None, :, None]
    outs = []
    for gi, win in enumerate(C_WINDOWS):
        c = cs[..., gi * D_C:(gi + 1) * D_C]
        prev = jnp.pad(c[:, :-win], ((0, 0), (win, 0), (0, 0)))
        mean = (c - prev) / jnp.minimum(t, jnp.float32(win))
        outs.append(mean - xf[..., gi * D_C:(gi + 1) * D_C])
    p = jnp.stack(outs, axis=2).astype(x.dtype)
    y = jnp.einsum('bsgc,gcd->bsgd', p, w_c).reshape(Bn, S, D_MODEL)
    return y * scale


def conv_ffn(x, w_up, b_up, conv_w, conv_b, w_down):
    h = x @ w_up + b_up
    h = causal_dwconv(h, conv_w) + conv_b
    g, v = jnp.split(h, 2, axis=-1)
    return (jax.nn.gelu(g) * v) @ w_down


def setup_inputs(seed: int = 0) -> dict:
    key = jax.random.key(seed)
    ks = jax.random.split(key, 20)
    f32 = jnp.float32
    nrm = lambda k, shape, s: jax.random.normal(k, shape, f32) * s
    return {
        "x": nrm(ks[0], (BATCH, SEQ, D_MODEL), 1.0),
        "w_in_even": nrm(ks[1], (N_EVEN, D_MODEL, D_IN_EVEN), D_MODEL ** -0.5),
        "gmlp_ws": nrm(ks[2], (N_EVEN, A_HEADS, GMLP_BLOCK, GMLP_BLOCK), 0.5 * GMLP_BLOCK ** -0.5),
        "gmlp_bs": 1.0 + nrm(ks[3], (N_EVEN, A_HEADS, GMLP_BLOCK), 0.02),
        "gmlp_ln_g": 1.0 + nrm(ks[4], (N_EVEN, D_A), 0.02),
        "gmlp_ln_b": nrm(ks[5], (N_EVEN, D_A), 0.02),
        "sconv_w": nrm(ks[6], (N_EVEN, B_CONV, D_B), B_CONV ** -0.5),
        "w_out_even": nrm(ks[7], (N_EVEN, D_A + D_B, D_MODEL), BETA * (D_A + D_B) ** -0.5),
        "pool_w": nrm(ks[8], (N_ODD, C_GROUPS, D_C, D_C), BETA * D_C ** -0.5),
        "pool_scale": 1.0 + nrm(ks[9], (N_ODD, D_MODEL), 0.02),
        "ffn_w_up": nrm(ks[10], (DEPTH, D_MODEL, 2 * D_FF), D_MODEL ** -0.5),
        "ffn_b_up": nrm(ks[11], (DEPTH, 2 * D_FF), 0.01),
        "ffn_conv_w": nrm(ks[12], (DEPTH, FFN_CONV, 2 * D_FF), FFN_CONV ** -0.5),
        "ffn_conv_b": nrm(ks[13], (DEPTH, 2 * D_FF), 0.01),
        "ffn_w_down": nrm(ks[14], (DEPTH, D_FF, D_MODEL), BETA * D_FF ** -0.5),
        "ln_mix_g": 1.0 + nrm(ks[15], (DEPTH, D_MODEL), 0.02),
        "ln_mix_b": nrm(ks[16], (DEPTH, D_MODEL), 0.02),
        "ln_ffn_g": 1.0 + nrm(ks[17], (DEPTH, D_MODEL), 0.02),
        "ln_ffn_b": nrm(ks[18], (DEPTH, D_MODEL), 0.02),
    }


def reference(x, w_in_even, gmlp_ws, gmlp_bs, gmlp_ln_g, gmlp_ln_b, sconv_w,
              w_out_even, pool_w, pool_scale, ffn_w_up, ffn_b_up, ffn_conv_w,
              ffn_conv_b, ffn_w_down, ln_mix_g, ln_mix_b, ln_ffn_g, ln_ffn_b):
    for layer in range(DEPTH):
        i = layer // 2
        if layer % 2 == 0:
            proj = x @ w_in_even[i]
            ya = gmlp_mixer(proj[..., :2 * D_A], gmlp_ws[i], gmlp_bs[i],
                            gmlp_ln_g[i], gmlp_ln_b[i])
            yb = shortconv_mixer(proj[..., 2 * D_A:], sconv_w[i])
            y = jnp.concatenate([ya, yb], axis=-1) @ w_out_even[i]
        else:
            y = pool_mixer(x, pool_w[i], pool_scale[i])
        x = layer_norm(ALPHA * x + y, ln_mix_g[layer], ln_mix_b[layer])
        f = conv_ffn(x, ffn_w_up[layer], ffn_b_up[layer], ffn_conv_w[layer],
                     ffn_conv_b[layer], ffn_w_down[layer])
        x = layer_norm(ALPHA * x + f, ln_ffn_g[layer], ln_ffn_b[layer])
    return x
```

```python
import contextlib
import numpy as np
import concourse.bass as bass
import concourse.mybir as mybir
from concourse.bass_utils import run_bass_kernel_spmd

F32 = mybir.dt.float32
BF16 = mybir.dt.bfloat16
I32 = mybir.dt.int32
AF = mybir.ActivationFunctionType
ALU = mybir.AluOpType

D_MODEL = 1024
SEQ = 4096
BATCH = 4
DEPTH = 4
KC = 8
D_FF = 2816
NFF = 22
ALPHA = (2.0 * DEPTH) ** 0.25
LN_EPS = 1e-5
T = 2176
OWN_A = 2088
NBLK = T // 128
TILES = [(0, 512), (512, 512), (1024, 512), (1536, 512), (2048, 128)]
UNITS = [(0, 1024, [0, 1]), (1024, 1024, [2, 3]), (2048, 128, [4])]
SLABS = [(0, 6), (6, 6), (12, 5), (17, 5)]
NA = 8
C_WINDOWS = (2, 4, 8, 16)

PL_LNMG, PL_LNMB, PL_LNFG, PL_LNFB = 0, 8, 16, 24
PL_BUP = 32
PL_CW = 76
PL_CB = 208
PL_PS = 252
PL_SW = 260
PL_N = 272
NPAR = PL_N * DEPTH


class Tick:
    __slots__ = ("eng", "seq", "sem", "val")

    def __init__(self, eng, seq, sem, val):
        self.eng, self.seq, self.sem, self.val = eng, seq, sem, val


class Buf:
    __slots__ = ("name", "arena", "lo", "hi", "w", "r")

    def __init__(self, name, arena=None, lo=0, hi=0):
        self.name, self.arena, self.lo, self.hi = name, arena, lo, hi
        self.w = None
        self.r = {}


class Eng:
    def __init__(self, name, sems, roll=12000):
        self.name = name
        self.sems = sems
        self.roll = roll
        self.seq = 0
        self.ops = []
        self.waited = {}
        self.pending = []

    def next_tick(self):
        self.seq += 1
        si = (self.seq - 1) // self.roll
        t = Tick(self.name, self.seq, self.sems[si], (self.seq - 1) % self.roll + 1)
        for p in self.pending:
            p.seq, p.sem, p.val = t.seq, t.sem, t.val
        self.pending = []
        return t


class Prog:
    def __init__(self):
        self.engs = {}
        self.arena_bufs = {}
        self.dma_ch = {}

    def add_engine(self, name, sems):
        self.engs[name] = Eng(name, sems)

    def add_dma_channel(self, name, sem):
        self.dma_ch[name] = [sem, 0]

    def buf(self, name, arena=None, lo=0, hi=0):
        b = Buf(name, arena, lo, hi)
        if arena is not None:
            self.arena_bufs.setdefault(arena, []).append(b)
        return b

    def _overl(self, b):
        if b.arena is None:
            return ()
        return [o for o in self.arena_bufs[b.arena]
                if o is not b and o.lo < b.hi and b.lo < o.hi and (o.w is not None or o.r)]

    def _deps(self, eng, reads, writes):
        raw, other = [], []
        for b in reads:
            if b.w is not None:
                raw.append(b.w)
            for o in self._overl(b):
                if o.w is not None:
                    raw.append(o.w)
        for b in writes:
            if b.w is not None:
                other.append(b.w)
            other.extend(b.r.values())
            for o in self._overl(b):
                if o.w is not None:
                    other.append(o.w)
                other.extend(o.r.values())
                o.w, o.r = None, {}
        return raw, other

    def _waits(self, e, raw, other):
        waits = []
        for lst, is_raw in ((raw, True), (other, False)):
            for t in lst:
                if t.eng == e.name:
                    if not is_raw or e.name == "pe":
                        continue
                if t.eng.startswith("dma:"):
                    key = t.eng
                    if e.waited.get(key, 0) >= t.val:
                        continue
                    e.waited[key] = t.val
                    waits.append((t.sem, t.val))
                else:
                    assert t.seq is not None, "unresolved deferred tick"
                    if e.waited.get(t.eng, 0) >= t.seq:
                        continue
                    e.waited[t.eng] = t.seq
                    waits.append((t.sem, t.val))
        return waits

    def gc(self):
        for a in self.arena_bufs:
            self.arena_bufs[a] = [b for b in self.arena_bufs[a] if b.w is not None or b.r]

    def op(self, eng, fn, reads=(), writes=(), inc=True):
        e = self.engs[eng]
        raw, other = self._deps(e, reads, writes)
        waits = self._waits(e, raw, other)
        if inc:
            tick = e.next_tick()
            e.ops.append((waits, fn, tick, 1))
        else:
            tick = Tick(eng, None, None, None)
            e.pending.append(tick)
            e.ops.append((waits, fn, None, 0))
        for b in reads:
            b.r[eng] = tick
        for b in writes:
            b.w, b.r = tick, {}
        return tick

    def dma(self, eng, ch, fn, reads=(), writes=()):
        e = self.engs[eng]
        raw, other = self._deps(e, reads, writes)
        waits = self._waits(e, raw, other)
        c = self.dma_ch[ch]
        c[1] += 16
        tick = Tick("dma:" + ch, c[1], c[0], c[1])
        e.ops.append((waits, fn, tick, 16))
        for b in reads:
            b.r["dma:" + ch] = tick
        for b in writes:
            b.w, b.r = tick, {}
        return tick

    def final_wait(self, eng, ticks):
        e = self.engs[eng]
        waits = self._waits(e, list(ticks), [])
        e.ops.append((waits, None, None, 0))

    def replay(self, eng, h):
        for waits, fn, tick, inc in self.engs[eng].ops:
            for sem, val in waits:
                h.wait_ge(sem, val)
            if fn is None:
                continue
            ins = fn(h)
            if tick is not None:
                ins.then_inc(tick.sem, inc)


def _items(W, chunk_order):
    n = len(chunk_order) // 2
    Wr = W.reshape(KC, 128, -1)
    out = np.empty((n, 128, KC, 256), np.float32)
    for i in range(n):
        for j in range(2):
            c = chunk_order[2 * i + j]
            out[i, :, :, j * 128:(j + 1) * 128] = Wr[:, :, c * 128:(c + 1) * 128].transpose(1, 0, 2)
    return out.reshape(n, 128, KC * 256)


def _fm(vec):
    return np.ascontiguousarray(np.asarray(vec, np.float32).reshape(-1, 128).T)


EVEN_IN_ORDER = [4, 5, 6, 7, 0, 1, 2, 3] + [c for j in range(4) for c in (16 + j, 12 + j, 8 + j)]


def prep_weights(inp):
    items, wdown = [], []
    par = np.zeros((128, NPAR), np.float32)
    gm = {}
    for l in range(DEPTH):
        i = l // 2
        if l % 2 == 0:
            items.append(_items(np.asarray(inp["w_in_even"][i]), EVEN_IN_ORDER))
            items.append(_items(np.asarray(inp["w_out_even"][i]), list(range(8))))
        else:
            pw = np.asarray(inp["pool_w"][i], np.float32)
            it = pw.reshape(4, 2, 128, 256).transpose(2, 0, 1, 3).reshape(1, 128, KC * 256)
            items.append(np.ascontiguousarray(it))
        items.append(_items(np.asarray(inp["ffn_w_up"][l]), [c for m in range(NFF) for c in (m, NFF + m)]))
        wd = np.asarray(inp["ffn_w_down"][l], np.float32).reshape(NFF, 128, D_MODEL)
        for (j0, nj) in SLABS:
            s = np.zeros((128, 6, D_MODEL), np.float32)
            s[:, :nj, :] = wd[j0:j0 + nj].transpose(1, 0, 2)
            wdown.append(s.reshape(1, 128, 6 * D_MODEL))
        o = l * PL_N
        par[:, o + PL_LNMG:o + PL_LNMG + 8] = _fm(inp["ln_mix_g"][l])
        par[:, o + PL_LNMB:o + PL_LNMB + 8] = _fm(inp["ln_mix_b"][l])
        par[:, o + PL_LNFG:o + PL_LNFG + 8] = _fm(inp["ln_ffn_g"][l])
        par[:, o + PL_LNFB:o + PL_LNFB + 8] = _fm(inp["ln_ffn_b"][l])
        par[:, o + PL_BUP:o + PL_BUP + 44] = _fm(inp["ffn_b_up"][l])
        cw = np.asarray(inp["ffn_conv_w"][l], np.float32)
        for k in range(3):
            par[:, o + PL_CW + 44 * k:o + PL_CW + 44 * (k + 1)] = _fm(cw[k])
        par[:, o + PL_CB:o + PL_CB + 44] = _fm(inp["ffn_conv_b"][l])
        if l % 2 == 1:
            par[:, o + PL_PS:o + PL_PS + 8] = _fm(inp["pool_scale"][i])
        else:
            sw = np.asarray(inp["sconv_w"][i], np.float32)
            for k in range(3):
                par[:, o + PL_SW + 4 * k:o + PL_SW + 4 * (k + 1)] = _fm(sw[k])
    gws = np.asarray(inp["gmlp_ws"], np.float32)
    gm["gws"] = np.ascontiguousarray(gws.transpose(0, 3, 1, 2))
    gbs = np.asarray(inp["gmlp_bs"], np.float32)
    gm["gbs"] = np.ascontiguousarray(np.broadcast_to(
        np.tile(gbs[:, None, :, None, :], (1, 1, 1, 4, 1)).reshape(2, 1, 4 * 512), (2, 128, 2048)))
    gm["glg"] = np.ascontiguousarray(np.broadcast_to(np.asarray(inp["gmlp_ln_g"], np.float32)[:, None, :], (2, 128, 512)))
    gm["glb"] = np.ascontiguousarray(np.broadcast_to(np.asarray(inp["gmlp_ln_b"], np.float32)[:, None, :], (2, 128, 512)))
    return dict(wstream=np.concatenate(items, 0), wdown=np.concatenate(wdown, 0), par=par, **gm)


def build_program(nlayers=DEPTH, stop_after_mixer=False, debug=None):
    nc = bass.Bass("TRN2", target_bir_lowering=False)
    n_items = sum((14 if l % 2 == 0 else 1) + NFF for l in range(DEPTH))
    xT = nc.dram_tensor("xT", [D_MODEL, T], F32, kind="ExternalInput").ap()
    wstream = nc.dram_tensor("wstream", [n_items, 128, KC * 256], F32, kind="ExternalInput").ap()
    wdown_d = nc.dram_tensor("wdown", [4 * DEPTH, 128, 6 * D_MODEL], F32, kind="ExternalInput").ap()
    par_d = nc.dram_tensor("par", [128, NPAR], F32, kind="ExternalInput").ap()
    gws_d = nc.dram_tensor("gws", [2, 128, 4, 128], F32, kind="ExternalInput").ap()
    gbs_d = nc.dram_tensor("gbs", [2, 128, 2048], F32, kind="ExternalInput").ap()
    glg_d = nc.dram_tensor("glg", [2, 128, 512], F32, kind="ExternalInput").ap()
    glb_d = nc.dram_tensor("glb", [2, 128, 512], F32, kind="ExternalInput").ap()
    outT = nc.dram_tensor("outT", [D_MODEL, T], F32, kind="ExternalOutput").ap()
    if debug:
        dbgb = nc.dram_tensor("dbgb", [128, KC * T], BF16, kind="ExternalOutput").ap()
        dbgf = nc.dram_tensor("dbgf", [128, KC * T], F32, kind="ExternalOutput").ap()

    es = contextlib.ExitStack()
    with es:
        def sb(name, shape, dt):
            return es.enter_context(nc.sbuf_tensor("sb_" + name, shape, dt))

        xf = sb("xf", [128, KC * T], F32)
        xb = sb("xb", [128, KC * T], BF16)
        wring = sb("wring", [128, 3 * KC * 256], BF16)
        par = sb("par", [128, NPAR], F32)
        ones = sb("ones", [128, 128], BF16)
        invc = sb("invc", [128, 16], F32)
        invi = sb("invi", [128, 16], I32)
        epsb = sb("epsb", [128, 1], F32)
        nhalf = sb("nhalf", [128, 1], F32)
        SCR = 22100
        scr = sb("scr", [128, SCR], F32)
        scrb = scr.bitcast(BF16)
        ps = es.enter_context(nc.psum_tensor("ps", [128, 8 * 512], F32))

        P = Prog()
        for en in ("pe", "act", "dve", "pool", "sp"):
            P.add_engine(en, [es.enter_context(nc.semaphore(f"s_{en}{i}")) for i in range(3)])
        for ch in ("par", "x", "out", "w0", "w1", "w2", "wd", "gm"):
            P.add_dma_channel(ch, es.enter_context(nc.semaphore(f"d_{ch}")))

        B_xf = [[P.buf(f"xf{c}_{u}") for u in range(3)] for c in range(KC)]
        B_xb = [[P.buf(f"xb{c}_{u}") for u in range(3)] for c in range(KC)]
        B_bank = [P.buf(f"bank{b}") for b in range(8)]
        B_wr = [P.buf(f"wr{s}") for s in range(3)]
        B_par = P.buf("par")
        B_const = P.buf("const")

        def xfa(c, s, n):
            return xf[:, c * T + s:c * T + s + n]

        def xba(c, s, n):
            return xb[:, c * T + s:c * T + s + n]

        def wra(slot, k, c0, n):
            o = slot * KC * 256 + k * 256 + c0
            return wring[:, o:o + n]

        def para(l, off, n=1):
            o = l * PL_N + off
            return par[:, o:o + n]

        def unit_of_tile(ti):
            return 0 if ti < 2 else (1 if ti < 4 else 2)

        class Arena:
            def __init__(self):
                self.off = 0

            def take(self, nbytes):
                o = self.off
                self.off += (nbytes + 7) // 8 * 8
                assert self.off <= SCR * 4, f"scratch overflow {self.off}"
                return o

        def sview(off, n, dt):
            if dt == F32:
                return scr[:, off // 4:off // 4 + n]
            return scrb[:, off // 2:off // 2 + n]

        def sbuf_(name, off, n, dt):
            nb = n * (4 if dt == F32 else 2)
            return P.buf(name, "scr", off, off + nb)

        wstate = dict(next_load=0, next_use=0)

        def w_load(idx):
            slot = idx % 3
            P.dma("pool", f"w{slot}",
                  lambda g, idx=idx, slot=slot: g.dma_start(
                      out=wring[:, slot * KC * 256:(slot + 1) * KC * 256], in_=wstream[idx]),
                  writes=[B_wr[slot]])

        def w_next(depth=2):
            i = wstate["next_use"]
            wstate["next_use"] += 1
            while wstate["next_load"] < min(i + 1 + depth, wstate["limit"]):
                w_load(wstate["next_load"])
                wstate["next_load"] += 1
            return i % 3

        lim = 0
        for l in range(nlayers):
            lim += (14 if l % 2 == 0 else 1)
            if not (stop_after_mixer and l == nlayers - 1):
                lim += NFF
        wstate["limit"] = lim

        pstate = dict(unit=0, bank=0)

        def next_unit_slot():
            s = pstate["unit"] % 4
            pstate["unit"] += 1
            return s

        def next_bank():
            b = pstate["bank"] % 8
            pstate["bank"] += 1
            return b

        def psa(bank, off, n):
            return ps[:, bank * 512 + off:bank * 512 + off + n]

        P.dma("sp", "par", lambda s: s.dma_start(out=par[:], in_=par_d), writes=[B_par])
        for c in range(KC):
            P.dma("sp", "x", lambda s, c=c: s.dma_start(out=xf[:, c * T:(c + 1) * T], in_=xT[c * 128:(c + 1) * 128, :]),
                  writes=B_xf[c])
        P.op("dve", lambda v: v.memset(ones[:], 1.0 / D_MODEL), writes=[B_const])
        P.op("dve", lambda v: v.memset(epsb[:], LN_EPS), writes=[B_const])
        P.op("dve", lambda v: v.memset(nhalf[:], -0.5), writes=[B_const])
        P.op("pool", lambda g: g.iota(invi[:], pattern=[[1, 16]], base=1, channel_multiplier=0), writes=[B_const])
        P.op("dve", lambda v: v.tensor_copy(out=invc[:], in_=invi[:]), reads=[B_const], writes=[B_const])
        P.op("dve", lambda v: v.reciprocal(out=invc[:], in_=invc[:]), reads=[B_const], writes=[B_const])
        xt = P.dma_ch["x"]
        for c in range(KC):
            for u in range(3):
                B_xf[c][u].w = Tick("dma:x", xt[1], xt[0], xt[1])
        for c in range(KC):
            for u, (us, un, _) in enumerate(UNITS):
                eng = "dve" if (c + u) % 2 == 0 else "act"
                if eng == "dve":
                    P.op("dve", lambda v, c=c, us=us, un=un: v.tensor_copy(out=xba(c, us, un), in_=xfa(c, us, un)),
                         reads=[B_xf[c][u]], writes=[B_xb[c][u]])
                else:
                    P.op("act", lambda a, c=c, us=us, un=un: a.activation(out=xba(c, us, un), in_=xfa(c, us, un), func=AF.Copy),
                         reads=[B_xf[c][u]], writes=[B_xb[c][u]])

        LN_BASE = 47104

        def ln_bufs():
            o_sq = LN_BASE
            o_mean = o_sq + KC * 1024 * 2
            o_vr = o_mean + T * 4
            assert o_vr + T * 4 <= SCR * 4
            d = dict(
                B_sq=[sbuf_(f"sq{c}", o_sq + c * 1024 * 2, 1024, BF16) for c in range(KC)],
                B_mean=[sbuf_(f"mean{u}", o_mean + UNITS[u][0] * 4, UNITS[u][1], F32) for u in range(3)],
                B_vr=[sbuf_(f"vr{u}", o_vr + UNITS[u][0] * 4, UNITS[u][1], F32) for u in range(3)],
                sq=lambda c, n: sview(o_sq + c * 1024 * 2, n, BF16),
                mean=lambda s_, n: sview(o_mean + s_ * 4, n, F32),
                vr=lambda s_, n: sview(o_vr + s_ * 4, n, F32))
            return d

        def ln_front(L, u):
            us, un, tl = UNITS[u]
            sq, mean, vr = L["sq"], L["mean"], L["vr"]
            for c in range(KC):
                P.op("act", lambda a, c=c: a.activation(out=xba(c, us, un), in_=xfa(c, us, un), func=AF.Copy),
                     reads=[B_xf[c][u]], writes=[B_xb[c][u]])
                P.op("act", lambda a, c=c: a.activation(out=sq(c, un), in_=xfa(c, us, un), func=AF.Square),
                     reads=[B_xf[c][u]], writes=[L["B_sq"][c]])
            sm, sq_slot = next_unit_slot(), next_unit_slot()
            for which, slot in ((0, sm), (1, sq_slot)):
                for bi, ti in enumerate(tl):
                    ts, tn = TILES[ti]
                    for c in range(KC):
                        rhs = xba(c, ts, tn) if which == 0 else sq(c, un)[:, ts - us:ts - us + tn]
                        rb = B_xb[c][u] if which == 0 else L["B_sq"][c]
                        P.op("pe", lambda t, slot=slot, bi=bi, tn=tn, rhs=rhs, c=c: t.matmul(
                            psa(2 * slot + bi, 0, tn), ones[:], rhs, start=(c == 0), stop=(c == KC - 1)),
                            reads=[rb, B_const], writes=[B_bank[2 * slot + bi]], inc=(c == KC - 1))
            bm = [B_bank[2 * sm + bi] for bi in range(len(tl))]
            bq = [B_bank[2 * sq_slot + bi] for bi in range(len(tl))]
            pm = ps[:, sm * 1024:sm * 1024 + un]
            pq = ps[:, sq_slot * 1024:sq_slot * 1024 + un]
            P.op("act", lambda a: a.activation(out=mean(us, un), in_=pm, func=AF.Copy), reads=bm, writes=[L["B_mean"][u]])
            P.op("act", lambda a: a.activation(out=vr(us, un), in_=pm, func=AF.Square), reads=bm, writes=[L["B_vr"][u]])
            P.op("dve", lambda v: v.scalar_tensor_tensor(out=vr(us, un), in0=pq, scalar=LN_EPS, in1=vr(us, un),
                                                         op0=ALU.add, op1=ALU.subtract),
                 reads=bq + [L["B_vr"][u]], writes=[L["B_vr"][u]])
            P.op("act", lambda a: a.activation(out=vr(us, un), in_=vr(us, un), func=AF.Sqrt),
                 reads=[L["B_vr"][u]], writes=[L["B_vr"][u]])
            P.op("dve", lambda v: v.reciprocal(out=vr(us, un), in_=vr(us, un)),
                 reads=[L["B_vr"][u]], writes=[L["B_vr"][u]])

        def ln_back(L, l, u, goff, boff, make_xb=True):
            us, un, tl = UNITS[u]
            mean, vr = L["mean"], L["vr"]
            for c in range(KC):
                eng = "pool" if (c >= 6 and un >= 512) else "dve"
                P.op(eng, lambda v, c=c: v.tensor_tensor(
                    out=xfa(c, us, un), in0=xfa(c, us, un), in1=mean(us, un), op=ALU.subtract),
                    reads=[B_xf[c][u], L["B_mean"][u]], writes=[B_xf[c][u]])
                P.op(eng, lambda v, c=c: v.tensor_tensor(
                    out=xfa(c, us, un), in0=xfa(c, us, un), in1=vr(us, un), op=ALU.mult),
                    reads=[B_xf[c][u], L["B_vr"][u]], writes=[B_xf[c][u]])
                if make_xb:
                    P.op("act", lambda a, c=c: a.activation(
                        out=xba(c, us, un), in_=xfa(c, us, un), func=AF.Identity, bias=para(l, boff + c),
                        scale=para(l, goff + c)),
                        reads=[B_xf[c][u], B_par], writes=[B_xb[c][u]])
                P.op("act", lambda a, c=c: a.activation(
                    out=xfa(c, us, un), in_=xfa(c, us, un), func=AF.Identity, bias=para(l, boff + c),
                    scale=para(l, goff + c)),
                    reads=[B_xf[c][u], B_par], writes=[B_xf[c][u]])

        def layer_norm(l, goff, boff, make_xb=True, L=None, fronts_done=()):
            if L is None:
                P.gc()
                L = ln_bufs()
            for u in range(3):
                if u not in fronts_done:
                    ln_front(L, u)
            for u in range(3):
                ln_back(L, l, u, goff, boff, make_xb)

        def proj_unit(slot_w, col0, u, slot_p, rhs_fn, rhs_bufs, nk=KC, kfn=None):
            us, un, tl = UNITS[u]
            for bi, ti in enumerate(tl):
                ts, tn = TILES[ti]
                for k in range(nk):
                    kk = k if kfn is None else kfn(k)
                    P.op("pe", lambda t, bi=bi, tn=tn, ts=ts, k=k, kk=kk: t.matmul(
                        psa(2 * slot_p + bi, 0, tn), wra(slot_w, kk, col0, 128), rhs_fn(k, ts, tn),
                        start=(k == 0), stop=(k == nk - 1)),
                        reads=[B_wr[slot_w], rhs_bufs(k, u, ti)], writes=[B_bank[2 * slot_p + bi]],
                        inc=(k == nk - 1))
            return [B_bank[2 * slot_p + bi] for bi in range(len(tl))]

        def ffn(l):
            P.gc()
            ar = Arena()
            o_a = ar.take(NA * T * 2)
            o_wd = ar.take(6 * D_MODEL * 2)
            o_h = [ar.take(2 * 1026 * 4) for _ in range(2)]
            o_c = [ar.take(2 * 1024 * 4) for _ in range(3)]
            B_a = [[sbuf_(f"a{s}_{u}", o_a + (s * T + UNITS[u][0]) * 2, UNITS[u][1], BF16) for u in range(3)] for s in range(NA)]
            B_wd = sbuf_("wd", o_wd, 6 * D_MODEL, BF16)
            B_h = [[sbuf_(f"h{q}_{gv}", o_h[q] + gv * 1026 * 4, 1026, F32) for gv in range(2)] for q in range(2)]
            B_c = [[sbuf_(f"c{q}_{gv}", o_c[q] + gv * 1024 * 4, 1024, F32) for gv in range(2)] for q in range(3)]
            aview = lambda s, c0, n: sview(o_a + (s * T + c0) * 2, n, BF16)
            wdv = lambda jj, c0, n: sview(o_wd + (jj * D_MODEL + c0) * 2, n, BF16)
            hv = lambda q, gv, c0, n: sview(o_h[q] + (gv * 1026 + c0) * 4, n, F32)
            cv = lambda q, gv, c0, n: sview(o_c[q] + (gv * 1024 + c0) * 4, n, F32)

            def load_wd(s):
                P.dma("pool", "wd", lambda g, s=s: g.dma_start(out=sview(o_wd, 6 * D_MODEL, BF16), in_=wdown_d[4 * l + s]),
                      writes=[B_wd])

            jobs = []
            st = dict(j=0, prev=None)

            def u_front(m, slot_w, u):
                us, un, tl = UNITS[u]
                q = st["j"] % 2
                qc = st["j"] % 3
                st["j"] += 1
                sg, sv_ = next_unit_slot(), next_unit_slot()
                bg = proj_unit(slot_w, 0, u, sg, lambda k, ts, tn: xba(k, ts, tn), lambda k, u, ti: B_xb[k][u])
                bv = proj_unit(slot_w, 128, u, sv_, lambda k, ts, tn: xba(k, ts, tn), lambda k, u, ti: B_xb[k][u])
                pg = ps[:, sg * 1024:sg * 1024 + un]
                pv = ps[:, sv_ * 1024:sv_ * 1024 + un]
                if u == 0:
                    for gv in range(2):
                        P.op("dve", lambda v, q=q, gv=gv: v.memset(hv(q, gv, 0, 2), 0.0), writes=[B_h[q][gv]])
                else:
                    pun = UNITS[u - 1][1]
                    for gv in range(2):
                        P.op("act", lambda a, q=q, gv=gv, pun=pun: a.activation(
                            out=hv(q, gv, 0, 2), in_=hv(q ^ 1, gv, pun, 2), func=AF.Copy),
                            reads=[B_h[q ^ 1][gv]], writes=[B_h[q][gv]])
                P.op("act", lambda a, q=q, pg=pg, un=un, m=m: a.activation(
                    out=hv(q, 0, 2, un), in_=pg, func=AF.Identity, bias=para(l, PL_BUP + m), scale=1.0),
                    reads=bg + [B_par], writes=[B_h[q][0]])
                P.op("act", lambda a, q=q, pv=pv, un=un, m=m: a.activation(
                    out=hv(q, 1, 2, un), in_=pv, func=AF.Identity, bias=para(l, PL_BUP + NFF + m), scale=1.0),
                    reads=bv + [B_par], writes=[B_h[q][1]])
                for gv in range(2):
                    ch = m + gv * NFF
                    P.op("act", lambda a, q=q, qc=qc, gv=gv, un=un, ch=ch: a.activation(
                        out=cv(qc, gv, 0, un), in_=hv(q, gv, 0, un), func=AF.Identity,
                        bias=para(l, PL_CB + ch), scale=para(l, PL_CW + ch)),
                        reads=[B_h[q][gv], B_par], writes=[B_c[qc][gv]])
                for gv in range(2):
                    ch = m + gv * NFF
                    for k in (1, 2):
                        P.op("dve", lambda v, q=q, qc=qc, gv=gv, un=un, ch=ch, k=k: v.scalar_tensor_tensor(
                            out=cv(qc, gv, 0, un), in0=hv(q, gv, k, un), scalar=para(l, PL_CW + 44 * k + ch),
                            in1=cv(qc, gv, 0, un), op0=ALU.mult, op1=ALU.add),
                            reads=[B_h[q][gv], B_c[qc][gv], B_par], writes=[B_c[qc][gv]])
                return (m, u, qc)

            def u_back(job):
                m, u, q = job
                us, un, tl = UNITS[u]
                aslot = m % NA
                P.op("act", lambda a, q=q, un=un: a.activation(out=cv(q, 0, 0, un), in_=cv(q, 0, 0, un), func=AF.Gelu_apprx_tanh),
                     reads=[B_c[q][0]], writes=[B_c[q][0]])
                P.op("dve", lambda v, q=q, un=un, us=us, aslot=aslot: v.tensor_tensor(
                    out=aview(aslot, us, un), in0=cv(q, 0, 0, un), in1=cv(q, 1, 0, un), op=ALU.mult),
                    reads=[B_c[q][0], B_c[q][1]], writes=[B_a[aslot][u]])

            def u_pair(m):
                slot_w = w_next()
                for u in range(3):
                    job = u_front(m, slot_w, u)
                    if st["prev"] is not None:
                        u_back(st["prev"])
                    st["prev"] = job

            def flush():
                if st["prev"] is not None:
                    u_back(st["prev"])
                    st["prev"] = None

            def d_slab(s, L=None):
                j0, nj = SLABS[s]
                for u, (us, un, tl) in enumerate(UNITS):
                    if L is not None and u > 0:
                        ln_front(L, u - 1)
                    for o in range(KC):
                        slot = next_unit_slot()
                        for bi, ti in enumerate(tl):
                            ts, tn = TILES[ti]
                            for jj in range(nj):
                                aslot = (j0 + jj) % NA
                                P.op("pe", lambda t, slot=slot, bi=bi, tn=tn, ts=ts, jj=jj, o=o, aslot=aslot: t.matmul(
                                    psa(2 * slot + bi, 0, tn), wdv(jj, o * 128, 128), aview(aslot, ts, tn),
                                    start=(jj == 0), stop=(jj == nj - 1)),
                                    reads=[B_wd, B_a[aslot][u]], writes=[B_bank[2 * slot + bi]], inc=(jj == nj - 1))
                        bb = [B_bank[2 * slot + bi] for bi in range(len(tl))]
                        pp = ps[:, slot * 1024:slot * 1024 + un]
                        if s == 0:
                            P.op("dve", lambda v, o=o, us=us, un=un, pp=pp: v.scalar_tensor_tensor(
                                out=xfa(o, us, un), in0=xfa(o, us, un), scalar=ALPHA, in1=pp, op0=ALU.mult, op1=ALU.add),
                                reads=bb + [B_xf[o][u]], writes=[B_xf[o][u]])
                        else:
                            P.op("dve", lambda v, o=o, us=us, un=un, pp=pp: v.tensor_tensor(
                                out=xfa(o, us, un), in0=pp, in1=xfa(o, us, un), op=ALU.add),
                                reads=bb + [B_xf[o][u]], writes=[B_xf[o][u]])

            load_wd(0)
            for s, (j0, nj) in enumerate(SLABS):
                for jj in range(nj):
                    u_pair(j0 + jj)
                    if s > 0 and jj == 1:
                        d_slab(s - 1)
                        load_wd(s)
            flush()
            L = ln_bufs()
            d_slab(len(SLABS) - 1, L)
            ln_front(L, 2)
            return L

        def mixer_even(l):
            i = l // 2
            P.gc()
            ar = Arena()
            o_cat = ar.take(KC * T * 2)
            o_vln = ar.take(NBLK * 512 * 2)
            o_wm = ar.take(4 * 128 * 2)
            o_bs = ar.take(2048 * 4)
            o_lg = ar.take(512 * 4)
            o_lb = ar.take(512 * 4)
            o_st = ar.take(NBLK * 4 * 6 * 4)
            o_mv = ar.take(NBLK * 4 * 2 * 4)
            o_sd = ar.take(NBLK * 4 * 4)
            o_tmp = [ar.take(514 * 4) for _ in range(6)]
            B_cat = [[sbuf_(f"cat{c}_{t}", o_cat + (c * T + TILES[t][0]) * 2, TILES[t][1], BF16) for t in range(5)] for c in range(KC)]
            B_vg = [sbuf_(f"vg{b}", o_cat + b * 512 * 4, 512, F32) for b in range(NBLK)]
            B_vln = [sbuf_(f"vln{b}", o_vln + b * 512 * 2, 512, BF16) for b in range(NBLK)]
            B_gm = sbuf_("gmw", o_wm, (o_lb + 2048 - o_wm) // 2, BF16)
            B_st = sbuf_("st", o_st, (o_sd + NBLK * 16 - o_st) // 4, F32)
            B_tmp = [sbuf_(f"tmp{k}", o_tmp[k], 514, F32) for k in range(6)]
            cat = lambda c, s, n: sview(o_cat + (c * T + s) * 2, n, BF16)
            vg = lambda b, c0, n: sview(o_cat + (b * 512 + c0) * 4, n, F32)
            vln = lambda b, c0, n: sview(o_vln + (b * 512 + c0) * 2, n, BF16)
            wm = lambda h: sview(o_wm + h * 128 * 2, 128, BF16)
            bsv = lambda h, n: sview(o_bs + h * 512 * 4, n, F32)
            tmp = lambda k, c0, n: sview(o_tmp[k] + c0 * 4, n, F32)
            stv = lambda b, h: sview(o_st + (b * 4 + h) * 6 * 4, 6, F32)
            mvv = lambda b, h, j: sview(o_mv + ((b * 4 + h) * 2 + j) * 4, 1, F32)

            P.op("dve", lambda v: v.memset(sview(o_wm, 512, BF16), 0.0), writes=[B_gm])
            wmt = sview(o_wm, 512, BF16)
            P.dma("pool", "gm", lambda g: g.dma_start(out=wmt[0:64, :], in_=gws_d[i, 0:64].rearrange("p h i -> p (h i)")),
                  writes=[B_gm])
            for h in range(4):
                P.dma("pool", "gm", lambda g, h=h: g.dma_start(out=wmt[64:128, h * 128 + 64:h * 128 + 128],
                                                                 in_=gws_d[i, 64:128, h, 64:128]), writes=[B_gm])
            P.dma("sp", "gm", lambda s: s.dma_start(out=sview(o_bs, 2048, F32), in_=gbs_d[i]), writes=[B_gm])
            P.dma("sp", "gm", lambda s: s.dma_start(out=sview(o_lg, 512, F32), in_=glg_d[i]), writes=[B_gm])
            P.dma("sp", "gm", lambda s: s.dma_start(out=sview(o_lb, 512, F32), in_=glb_d[i]), writes=[B_gm])
            gt = P.dma_ch["gm"]
            B_gm.w = Tick("dma:gm", gt[1], gt[0], gt[1])

            sv0, sv1 = w_next(1), w_next(1)
            for b in range(NBLK):
                u = unit_of_tile(b // 4)
                bank = next_bank()
                for it, slot_w in enumerate((sv0, sv1)):
                    for k in range(KC):
                        P.op("pe", lambda t, bank=bank, it=it, slot_w=slot_w, k=k, b=b: t.matmul(
                            psa(bank, it * 256, 256), xba(k, b * 128, 128), wra(slot_w, k, 0, 256),
                            start=(k == 0), stop=(k == KC - 1)),
                            reads=[B_wr[slot_w], B_xb[k][u]], writes=[B_bank[bank]], inc=(k == KC - 1))
                P.op("act", lambda a, bank=bank, b=b: a.activation(out=vg(b, 0, 512), in_=psa(bank, 0, 512), func=AF.Gelu_apprx_tanh),
                     reads=[B_bank[bank]], writes=[B_vg[b]])
                for h in range(4):
                    P.op("dve", lambda v, b=b, h=h: v.bn_stats(out=stv(b, h), in_=vg(b, h * 128, 128)),
                         reads=[B_vg[b]], writes=[B_st])
                for h in range(4):
                    P.op("dve", lambda v, b=b, h=h: v.bn_aggr(out=sview(o_mv + (b * 4 + h) * 8, 2, F32), in_=stv(b, h)),
                         reads=[B_st], writes=[B_st])
            var_all = scr[:, o_mv // 4 + 1:o_mv // 4 + 1 + 2 * NBLK * 4:2]
            P.op("act", lambda a: a.activation(out=sview(o_sd, NBLK * 4, F32), in_=var_all, func=AF.Sqrt, bias=epsb[:, 0:1], scale=1.0),
                 reads=[B_st, B_const], writes=[B_st])
            P.op("dve", lambda v: v.reciprocal(out=sview(o_sd, NBLK * 4, F32), in_=sview(o_sd, NBLK * 4, F32)),
                 reads=[B_st], writes=[B_st])
            for b in range(NBLK):
                for h in range(4):
                    P.op("dve", lambda v, b=b, h=h: v.tensor_scalar(
                        out=vg(b, h * 128, 128), in0=vg(b, h * 128, 128), scalar1=mvv(b, h, 0),
                        scalar2=sview(o_sd + (b * 4 + h) * 4, 1, F32), op0=ALU.subtract, op1=ALU.mult),
                        reads=[B_vg[b], B_st], writes=[B_vg[b]])
                P.op("dve", lambda v, b=b: v.tensor_tensor(out=vg(b, 0, 512), in0=vg(b, 0, 512), in1=sview(o_lg, 512, F32), op=ALU.mult),
                     reads=[B_vg[b], B_gm], writes=[B_vg[b]])
                P.op("dve", lambda v, b=b: v.tensor_tensor(out=vln(b, 0, 512), in0=vg(b, 0, 512), in1=sview(o_lb, 512, F32), op=ALU.add),
                     reads=[B_vg[b], B_gm], writes=[B_vln[b]])

            if debug == "vln" and l == 0:
                P.dma("sp", "out", lambda s_: s_.dma_start(out=dbgb[:, 0:NBLK * 512], in_=sview(o_vln, NBLK * 512, BF16)),
                      reads=B_vln)
            tq = dict(q=0)
            for hp in range(2):
                slot_w = w_next()
                for hh in range(2):
                    h = 2 * hp + hh
                    for ti, (ts, tn) in enumerate(TILES):
                        u = unit_of_tile(ti)
                        q = tq["q"]
                        tq["q"] ^= 1
                        bu, bs_ = next_bank(), next_bank()
                        for k in range(KC):
                            P.op("pe", lambda t, bu=bu, k=k, ts=ts, tn=tn, hh=hh, slot_w=slot_w: t.matmul(
                                psa(bu, 0, tn), wra(slot_w, k, hh * 128, 128), xba(k, ts, tn),
                                start=(k == 0), stop=(k == KC - 1)),
                                reads=[B_wr[slot_w], B_xb[k][u]], writes=[B_bank[bu]], inc=(k == KC - 1))
                        for bi in range(tn // 128):
                            b = ts // 128 + bi
                            P.op("pe", lambda t, bs_=bs_, bi=bi, b=b, h=h: t.matmul(
                                psa(bs_, bi * 128, 128), vln(b, h * 128, 128), wm(h), start=True, stop=True),
                                reads=[B_vln[b], B_gm], writes=[B_bank[bs_]], inc=(bi == tn // 128 - 1))
                        P.op("act", lambda a, q=q, bu=bu, tn=tn: a.activation(out=tmp(q, 0, tn), in_=psa(bu, 0, tn), func=AF.Gelu_apprx_tanh),
                             reads=[B_bank[bu]], writes=[B_tmp[q]])
                        P.op("dve", lambda v, q=q, bs_=bs_, tn=tn, h=h: v.tensor_tensor(
                            out=tmp(2 + q, 0, tn), in0=psa(bs_, 0, tn), in1=bsv(h, tn), op=ALU.add),
                            reads=[B_bank[bs_], B_gm], writes=[B_tmp[2 + q]])
                        P.op("dve", lambda v, q=q, tn=tn, ts=ts, h=h: v.tensor_tensor(
                            out=cat(h, ts, tn), in0=tmp(2 + q, 0, tn), in1=tmp(q, 0, tn), op=ALU.mult),
                            reads=[B_tmp[q], B_tmp[2 + q]], writes=[B_cat[h][ti]])

            cur = dict(slot=None, pos=2)

            def next_chunk():
                if cur["pos"] == 2:
                    cur["slot"] = w_next(1)
                    cur["pos"] = 0
                r = (cur["slot"], cur["pos"] * 128)
                cur["pos"] += 1
                return r

            sq_ = dict(q=0)
            for j in range(4):
                wh, wgc, wgb = next_chunk(), next_chunk(), next_chunk()
                for ti, (ts, tn) in enumerate(TILES):
                    u = unit_of_tile(ti)
                    q = sq_["q"]
                    sq_["q"] ^= 1
                    banks = []
                    for (slot_w, c0) in (wh, wgc, wgb):
                        bk = next_bank()
                        banks.append(bk)
                        for k in range(KC):
                            P.op("pe", lambda t, bk=bk, k=k, ts=ts, tn=tn, slot_w=slot_w, c0=c0: t.matmul(
                                psa(bk, 0, tn), wra(slot_w, k, c0, 128), xba(k, ts, tn),
                                start=(k == 0), stop=(k == KC - 1)),
                                reads=[B_wr[slot_w], B_xb[k][u]], writes=[B_bank[bk]], inc=(k == KC - 1))
                    bh, bgc, bgb = banks
                    P.op("act", lambda a, q=q, bh=bh, tn=tn: a.activation(out=tmp(q, 0, tn), in_=psa(bh, 0, tn), func=AF.Copy),
                         reads=[B_bank[bh]], writes=[B_tmp[q]])
                    if ti == 0:
                        P.op("dve", lambda v, q=q: v.memset(tmp(2 + q, 0, 2), 0.0), writes=[B_tmp[2 + q]])
                    else:
                        ptn = TILES[ti - 1][1]
                        P.op("act", lambda a, q=q, ptn=ptn: a.activation(out=tmp(2 + q, 0, 2), in_=tmp(2 + (q ^ 1), ptn, 2), func=AF.Copy),
                             reads=[B_tmp[2 + (q ^ 1)]], writes=[B_tmp[2 + q]])
                    P.op("dve", lambda v, q=q, bgc=bgc, tn=tn: v.tensor_tensor(
                        out=tmp(2 + q, 2, tn), in0=psa(bgc, 0, tn), in1=tmp(q, 0, tn), op=ALU.mult),
                        reads=[B_bank[bgc], B_tmp[q]], writes=[B_tmp[2 + q]])
                    P.op("dve", lambda v, q=q, tn=tn, j=j: v.tensor_scalar(
                        out=tmp(4 + q, 0, tn), in0=tmp(2 + q, 0, tn), scalar1=para(l, PL_SW + j), scalar2=None, op0=ALU.mult),
                        reads=[B_tmp[2 + q], B_par], writes=[B_tmp[4 + q]])
                    for k in (1, 2):
                        P.op("dve", lambda v, q=q, tn=tn, j=j, k=k: v.scalar_tensor_tensor(
                            out=tmp(4 + q, 0, tn), in0=tmp(2 + q, k, tn), scalar=para(l, PL_SW + 4 * k + j),
                            in1=tmp(4 + q, 0, tn), op0=ALU.mult, op1=ALU.add),
                            reads=[B_tmp[2 + q], B_tmp[4 + q], B_par], writes=[B_tmp[4 + q]])
                    P.op("dve", lambda v, q=q, bgb=bgb, tn=tn, ts=ts, j=j: v.tensor_tensor(
                        out=cat(4 + j, ts, tn), in0=psa(bgb, 0, tn), in1=tmp(4 + q, 0, tn), op=ALU.mult),
                        reads=[B_bank[bgb], B_tmp[4 + q]], writes=[B_cat[4 + j][ti]])

            if debug == "cat" and l == 0:
                P.dma("sp", "out", lambda s_: s_.dma_start(out=dbgb, in_=sview(o_cat, KC * T, BF16)),
                      reads=[b_ for row in B_cat for b_ in row])
            for it in range(4):
                slot_w = w_next()
                for mo in range(2):
                    o = 2 * it + mo
                    for u, (us, un, tl) in enumerate(UNITS):
                        slot = next_unit_slot()
                        bb = proj_unit(slot_w, mo * 128, u, slot, lambda k, ts, tn: cat(k, ts, tn),
                                       lambda k, u, ti: B_cat[k][ti])
                        pp = ps[:, slot * 1024:slot * 1024 + un]
                        P.op("dve", lambda v, o=o, us=us, un=un, pp=pp: v.scalar_tensor_tensor(
                            out=xfa(o, us, un), in0=xfa(o, us, un), scalar=ALPHA, in1=pp, op0=ALU.mult, op1=ALU.add),
                            reads=bb + [B_xf[o][u]], writes=[B_xf[o][u]])

        def mixer_odd(l):
            i = l // 2
            P.gc()
            ar = Arena()
            o_p = ar.take(KC * T * 2)
            o_s = [ar.take(T * 4) for _ in range(4)]
            o_t = ar.take(16 * 4)
            B_p = [[sbuf_(f"p{c}_{u}", o_p + (c * T + UNITS[u][0]) * 2, UNITS[u][1], BF16) for u in range(3)] for c in range(KC)]
            B_s = [sbuf_(f"S{k}", o_s[k], T, F32) for k in range(4)]
            B_t = sbuf_("ptmp", o_t, 16, F32)
            pv = lambda c, s, n: sview(o_p + (c * T + s) * 2, n, BF16)
            sv = lambda k, s, n: sview(o_s[k] + s * 4, n, F32)
            slot_w = w_next()
            for c in range(KC):
                g = c // 2
                w = C_WINDOWS[g]
                src = None
                cur = 0
                sbase = 2 if g == 2 else 0
                sh = 1
                while sh < w:
                    dst = sbase + cur
                    if src is None:
                        a_full = lambda s, n, c=c: xfa(c, s, n)
                        rb = list(B_xf[c])
                    else:
                        a_full = lambda s, n, k=src: sv(k, s, n)
                        rb = [B_s[src]]
                    seng = "pool" if g == 2 else "dve"
                    P.op(seng, lambda v, a_full=a_full, dst=dst, sh=sh: v.tensor_tensor(
                        out=sv(dst, sh, T - sh), in0=a_full(sh, T - sh), in1=a_full(0, T - sh), op=ALU.add),
                        reads=rb, writes=[B_s[dst]])
                    P.op("act", lambda a, a_full=a_full, dst=dst, sh=sh: a.activation(out=sv(dst, 0, sh), in_=a_full(0, sh), func=AF.Copy),
                         reads=rb, writes=[B_s[dst]])
                    src = dst
                    cur ^= 1
                    sh *= 2
                P.op("dve", lambda v, c=c, src=src, w=w: v.scalar_tensor_tensor(
                    out=pv(c, 0, T), in0=sv(src, 0, T), scalar=1.0 / w, in1=xfa(c, 0, T), op0=ALU.mult, op1=ALU.subtract),
                    reads=[B_s[src]] + list(B_xf[c]), writes=list(B_p[c]))
                P.op("dve", lambda v, src=src, w=w: v.tensor_tensor(
                    out=sview(o_t, w - 1, F32), in0=sv(src, 0, w - 1), in1=invc[:, 0:w - 1], op=ALU.mult),
                    reads=[B_s[src], B_const], writes=[B_t])
                P.op("dve", lambda v, c=c, w=w: v.tensor_tensor(
                    out=pv(c, 0, w - 1), in0=sview(o_t, w - 1, F32), in1=xfa(c, 0, w - 1), op=ALU.subtract),
                    reads=[B_t, B_xf[c][0]], writes=[B_p[c][0]])
            for g in range(4):
                for mo in range(2):
                    o = 2 * g + mo
                    for u, (us, un, tl) in enumerate(UNITS):
                        slot = next_unit_slot()
                        bb = proj_unit(slot_w, mo * 128, u, slot, lambda k, ts, tn, g=g: pv(2 * g + k, ts, tn),
                                       lambda k, u, ti, g=g: B_p[2 * g + k][u], nk=2, kfn=lambda k, g=g: 2 * g + k)
                        pp = ps[:, slot * 1024:slot * 1024 + un]
                        P.op("dve", lambda v, o=o, us=us, un=un: v.tensor_scalar(
                            out=xfa(o, us, un), in0=xfa(o, us, un), scalar1=ALPHA, scalar2=None, op0=ALU.mult),
                            reads=[B_xf[o][u]], writes=[B_xf[o][u]])
                        P.op("dve", lambda v, o=o, us=us, un=un, pp=pp: v.scalar_tensor_tensor(
                            out=xfa(o, us, un), in0=pp, scalar=para(l, PL_PS + o), in1=xfa(o, us, un), op0=ALU.mult, op1=ALU.add),
                            reads=bb + [B_xf[o][u], B_par], writes=[B_xf[o][u]])

        for l in range(nlayers):
            last = (l == nlayers - 1)
            if l % 2 == 0:
                mixer_even(l)
            else:
                mixer_odd(l)
            layer_norm(l, PL_LNMG, PL_LNMB, make_xb=not (last and stop_after_mixer))
            if last and stop_after_mixer:
                break
            L = ffn(l)
            layer_norm(l, PL_LNFG, PL_LNFB, make_xb=not last, L=L, fronts_done=(0, 1, 2))

        for u, (us, un, tl) in enumerate(UNITS):
            for c in range(KC):
                P.dma("sp", "out", lambda s, c=c, us=us, un=un: s.dma_start(
                    out=outT[c * 128:(c + 1) * 128, us:us + un], in_=xfa(c, us, un)), reads=[B_xf[c][u]])
        ot = P.dma_ch["out"]
        P.final_wait("sp", [Tick("dma:out", ot[1], ot[0], ot[1])])

        with nc.Block() as block:
            @block.tensor
            def _(h):
                P.replay("pe", h)

            @block.scalar
            def _(h):
                P.replay("act", h)

            @block.vector
            def _(h):
                P.replay("dve", h)

            @block.gpsimd
            def _(h):
                P.replay("pool", h)

            @block.sync
            def _(h):
                P.replay("sp", h)
    return nc


def core_slices():
    sl = []
    for b in range(BATCH):
        sl.append((b, 0, T))
        sl.append((b, SEQ - T, SEQ))
    return sl


def kernel(**inputs):
    x = np.asarray(inputs["x"], np.float32)
    w = prep_weights(inputs)
    in_maps = []
    for (b, s, e) in core_slices():
        m = dict(w)
        m["xT"] = np.ascontiguousarray(x[b, s:e, :].T)
        in_maps.append(m)
    nc = build_program()
    res = run_bass_kernel_spmd(nc, in_maps, core_ids=list(range(8)))
    out = np.empty((BATCH, SEQ, D_MODEL), np.float32)
    for ci, (b, s, e) in enumerate(core_slices()):
        oT = res.results[ci]["outT"]
        if s == 0:
            out[b, 0:OWN_A, :] = oT[:, 0:OWN_A].T
        else:
            out[b, OWN_A:SEQ, :] = oT[:, OWN_A - s:T].T
    return out
```

```python
import contextlib
import numpy as np
import concourse.bass as bass
import concourse.mybir as mybir
from concourse.bass_utils import run_bass_kernel_spmd

F32 = mybir.dt.float32
BF16 = mybir.dt.bfloat16
I32 = mybir.dt.int32
AF = mybir.ActivationFunctionType
ALU = mybir.AluOpType

D_MODEL = 1024
SEQ = 4096
BATCH = 4
DEPTH = 4
KC = 8
D_FF = 2816
NFF = 22
ALPHA = (2.0 * DEPTH) ** 0.25
LN_EPS = 1e-5
T = 2176
OWN_A = 2088
NBLK = T // 128
TILES = [(0, 512), (512, 512), (1024, 512), (1536, 512), (2048, 128)]
UNITS = [(0, 1024, [0, 1]), (1024, 1024, [2, 3]), (2048, 128, [4])]
SLABS = [(0, 6), (6, 6), (12, 5), (17, 5)]
NA = 8
C_WINDOWS = (2, 4, 8, 16)

PL_LNMG, PL_LNMB, PL_LNFG, PL_LNFB = 0, 8, 16, 24
PL_BUP = 32
PL_CW = 76
PL_CB = 208
PL_PS = 252
PL_SW = 260
PL_N = 272
NPAR = PL_N * DEPTH


class Tick:
    __slots__ = ("eng", "seq", "sem", "val")

    def __init__(self, eng, seq, sem, val):
        self.eng, self.seq, self.sem, self.val = eng, seq, sem, val


class Buf:
    __slots__ = ("name", "arena", "lo", "hi", "w", "r")

    def __init__(self, name, arena=None, lo=0, hi=0):
        self.name, self.arena, self.lo, self.hi = name, arena, lo, hi
        self.w = None
        self.r = {}


class Eng:
    def __init__(self, name, sems, roll=12000):
        self.name = name
        self.sems = sems
        self.roll = roll
        self.seq = 0
        self.ops = []
        self.waited = {}
        self.pending = []

    def next_tick(self):
        self.seq += 1
        si = (self.seq - 1) // self.roll
        t = Tick(self.name, self.seq, self.sems[si], (self.seq - 1) % self.roll + 1)
        for p in self.pending:
            p.seq, p.sem, p.val = t.seq, t.sem, t.val
        self.pending = []
        return t


class Prog:
    def __init__(self):
        self.engs = {}
        self.arena_bufs = {}
        self.dma_ch = {}

    def add_engine(self, name, sems):
        self.engs[name] = Eng(name, sems)

    def add_dma_channel(self, name, sem):
        self.dma_ch[name] = [sem, 0]

    def buf(self, name, arena=None, lo=0, hi=0):
        b = Buf(name, arena, lo, hi)
        if arena is not None:
            self.arena_bufs.setdefault(arena, []).append(b)
        return b

    def _overl(self, b):
        if b.arena is None:
            return ()
        return [o for o in self.arena_bufs[b.arena]
                if o is not b and o.lo < b.hi and b.lo < o.hi and (o.w is not None or o.r)]

    def _deps(self, eng, reads, writes):
        raw, other = [], []
        for b in reads:
            if b.w is not None:
                raw.append(b.w)
            for o in self._overl(b):
                if o.w is not None:
                    raw.append(o.w)
        for b in writes:
            if b.w is not None:
                other.append(b.w)
            other.extend(b.r.values())
            for o in self._overl(b):
                if o.w is not None:
                    other.append(o.w)
                other.extend(o.r.values())
                o.w, o.r = None, {}
        return raw, other

    def _waits(self, e, raw, other):
        waits = []
        for lst, is_raw in ((raw, True), (other, False)):
            for t in lst:
                if t.eng == e.name:
                    if not is_raw or e.name == "pe":
                        continue
                if t.eng.startswith("dma:"):
                    key = t.eng
                    if e.waited.get(key, 0) >= t.val:
                        continue
                    e.waited[key] = t.val
                    waits.append((t.sem, t.val))
                else:
                    assert t.seq is not None, "unresolved deferred tick"
                    if e.waited.get(t.eng, 0) >= t.seq:
                        continue
                    e.waited[t.eng] = t.seq
                    waits.append((t.sem, t.val))
        return waits

    def gc(self):
        for a in self.arena_bufs:
            self.arena_bufs[a] = [b for b in self.arena_bufs[a] if b.w is not None or b.r]

    def op(self, eng, fn, reads=(), writes=(), inc=True):
        e = self.engs[eng]
        raw, other = self._deps(e, reads, writes)
        waits = self._waits(e, raw, other)
        if inc:
            tick = e.next_tick()
            e.ops.append((waits, fn, tick, 1))
        else:
            tick = Tick(eng, None, None, None)
            e.pending.append(tick)
            e.ops.append((waits, fn, None, 0))
        for b in reads:
            b.r[eng] = tick
        for b in writes:
            b.w, b.r = tick, {}
        return tick

    def dma(self, eng, ch, fn, reads=(), writes=()):
        e = self.engs[eng]
        raw, other = self._deps(e, reads, writes)
        waits = self._waits(e, raw, other)
        c = self.dma_ch[ch]
        c[1] += 16
        tick = Tick("dma:" + ch, c[1], c[0], c[1])
        e.ops.append((waits, fn, tick, 16))
        for b in reads:
            b.r["dma:" + ch] = tick
        for b in writes:
            b.w, b.r = tick, {}
        return tick

    def final_wait(self, eng, ticks):
        e = self.engs[eng]
        waits = self._waits(e, list(ticks), [])
        e.ops.append((waits, None, None, 0))

    def replay(self, eng, h):
        for waits, fn, tick, inc in self.engs[eng].ops:
            for sem, val in waits:
                h.wait_ge(sem, val)
            if fn is None:
                continue
            ins = fn(h)
            if tick is not None:
                ins.then_inc(tick.sem, inc)


def _items(W, chunk_order):
    n = len(chunk_order) // 2
    Wr = W.reshape(KC, 128, -1)
    out = np.empty((n, 128, KC, 256), np.float32)
    for i in range(n):
        for j in range(2):
            c = chunk_order[2 * i + j]
            out[i, :, :, j * 128:(j + 1) * 128] = Wr[:, :, c * 128:(c + 1) * 128].transpose(1, 0, 2)
    return out.reshape(n, 128, KC * 256)


def _fm(vec):
    return np.ascontiguousarray(np.asarray(vec, np.float32).reshape(-1, 128).T)


EVEN_IN_ORDER = [4, 5, 6, 7, 0, 1, 2, 3] + [c for j in range(4) for c in (16 + j, 12 + j, 8 + j)]


def prep_weights(inp):
    items, wdown = [], []
    par = np.zeros((128, NPAR), np.float32)
    gm = {}
    for l in range(DEPTH):
        i = l // 2
        if l % 2 == 0:
            items.append(_items(np.asarray(inp["w_in_even"][i]), EVEN_IN_ORDER))
            items.append(_items(np.asarray(inp["w_out_even"][i]), list(range(8))))
        else:
            pw = np.asarray(inp["pool_w"][i], np.float32)
            it = pw.reshape(4, 2, 128, 256).transpose(2, 0, 1, 3).reshape(1, 128, KC * 256)
            items.append(np.ascontiguousarray(it))
        items.append(_items(np.asarray(inp["ffn_w_up"][l]), [c for m in range(NFF) for c in (m, NFF + m)]))
        wd = np.asarray(inp["ffn_w_down"][l], np.float32).reshape(NFF, 128, D_MODEL)
        for (j0, nj) in SLABS:
            s = np.zeros((128, 6, D_MODEL), np.float32)
            s[:, :nj, :] = wd[j0:j0 + nj].transpose(1, 0, 2)
            wdown.append(s.reshape(1, 128, 6 * D_MODEL))
        o = l * PL_N
        par[:, o + PL_LNMG:o + PL_LNMG + 8] = _fm(inp["ln_mix_g"][l])
        par[:, o + PL_LNMB:o + PL_LNMB + 8] = _fm(inp["ln_mix_b"][l])
        par[:, o + PL_LNFG:o + PL_LNFG + 8] = _fm(inp["ln_ffn_g"][l])
        par[:, o + PL_LNFB:o + PL_LNFB + 8] = _fm(inp["ln_ffn_b"][l])
        par[:, o + PL_BUP:o + PL_BUP + 44] = _fm(inp["ffn_b_up"][l])
        cw = np.asarray(inp["ffn_conv_w"][l], np.float32)
        for k in range(3):
            par[:, o + PL_CW + 44 * k:o + PL_CW + 44 * (k + 1)] = _fm(cw[k])
        par[:, o + PL_CB:o + PL_CB + 44] = _fm(inp["ffn_conv_b"][l])
        if l % 2 == 1:
            par[:, o + PL_PS:o + PL_PS + 8] = _fm(inp["pool_scale"][i])
        else:
            sw = np.asarray(inp["sconv_w"][i], np.float32)
            for k in range(3):
                par[:, o + PL_SW + 4 * k:o + PL_SW + 4 * (k + 1)] = _fm(sw[k])
    gws = np.asarray(inp["gmlp_ws"], np.float32)
    gm["gws"] = np.ascontiguousarray(gws.transpose(0, 3, 1, 2))
    gbs = np.asarray(inp["gmlp_bs"], np.float32)
    gm["gbs"] = np.ascontiguousarray(np.broadcast_to(
        np.tile(gbs[:, None, :, None, :], (1, 1, 1, 4, 1)).reshape(2, 1, 4 * 512), (2, 128, 2048)))
    gm["glg"] = np.ascontiguousarray(np.broadcast_to(np.asarray(inp["gmlp_ln_g"], np.float32)[:, None, :], (2, 128, 512)))
    gm["glb"] = np.ascontiguousarray(np.broadcast_to(np.asarray(inp["gmlp_ln_b"], np.float32)[:, None, :], (2, 128, 512)))
    return dict(wstream=np.concatenate(items, 0), wdown=np.concatenate(wdown, 0), par=par, **gm)


def build_program(nlayers=DEPTH, stop_after_mixer=False, debug=None):
    nc = bass.Bass("TRN2", target_bir_lowering=False)
    n_items = sum((14 if l % 2 == 0 else 1) + NFF for l in range(DEPTH))
    xT = nc.dram_tensor("xT", [D_MODEL, T], F32, kind="ExternalInput").ap()
    wstream = nc.dram_tensor("wstream", [n_items, 128, KC * 256], F32, kind="ExternalInput").ap()
    wdown_d = nc.dram_tensor("wdown", [4 * DEPTH, 128, 6 * D_MODEL], F32, kind="ExternalInput").ap()
    par_d = nc.dram_tensor("par", [128, NPAR], F32, kind="ExternalInput").ap()
    gws_d = nc.dram_tensor("gws", [2, 128, 4, 128], F32, kind="ExternalInput").ap()
    gbs_d = nc.dram_tensor("gbs", [2, 128, 2048], F32, kind="ExternalInput").ap()
    glg_d = nc.dram_tensor("glg", [2, 128, 512], F32, kind="ExternalInput").ap()
    glb_d = nc.dram_tensor("glb", [2, 128, 512], F32, kind="ExternalInput").ap()
    outT = nc.dram_tensor("outT", [D_MODEL, T], F32, kind="ExternalOutput").ap()
    if debug:
        dbgb = nc.dram_tensor("dbgb", [128, KC * T], BF16, kind="ExternalOutput").ap()
        dbgf = nc.dram_tensor("dbgf", [128, KC * T], F32, kind="ExternalOutput").ap()

    es = contextlib.ExitStack()
    with es:
        def sb(name, shape, dt):
            return es.enter_context(nc.sbuf_tensor("sb_" + name, shape, dt))

        xf = sb("xf", [128, KC * T], F32)
        xb = sb("xb", [128, KC * T], BF16)
        wring = sb("wring", [128, 3 * KC * 256], BF16)
        par = sb("par", [128, NPAR], F32)
        ones = sb("ones", [128, 128], BF16)
        invc = sb("invc", [128, 16], F32)
        invi = sb("invi", [128, 16], I32)
        epsb = sb("epsb", [128, 1], F32)
        nhalf = sb("nhalf", [128, 1], F32)
        SCR = 22100
        scr = sb("scr", [128, SCR], F32)
        scrb = scr.bitcast(BF16)
        ps = es.enter_context(nc.psum_tensor("ps", [128, 8 * 512], F32))

        P = Prog()
        for en in ("pe", "act", "dve", "pool", "sp"):
            P.add_engine(en, [es.enter_context(nc.semaphore(f"s_{en}{i}")) for i in range(3)])
        for ch in ("par", "x", "out", "w0", "w1", "w2", "wd", "gm"):
            P.add_dma_channel(ch, es.enter_context(nc.semaphore(f"d_{ch}")))

        B_xf = [[P.buf(f"xf{c}_{u}") for u in range(3)] for c in range(KC)]
        B_xb = [[P.buf(f"xb{c}_{u}") for u in range(3)] for c in range(KC)]
        B_bank = [P.buf(f"bank{b}") for b in range(8)]
        B_wr = [P.buf(f"wr{s}") for s in range(3)]
        B_par = P.buf("par")
        B_const = P.buf("const")

        def xfa(c, s, n):
            return xf[:, c * T + s:c * T + s + n]

        def xba(c, s, n):
            return xb[:, c * T + s:c * T + s + n]

        def wra(slot, k, c0, n):
            o = slot * KC * 256 + k * 256 + c0
            return wring[:, o:o + n]

        def para(l, off, n=1):
            o = l * PL_N + off
            return par[:, o:o + n]

        def unit_of_tile(ti):
            return 0 if ti < 2 else (1 if ti < 4 else 2)

        class Arena:
            def __init__(self):
                self.off = 0

            def take(self, nbytes):
                o = self.off
                self.off += (nbytes + 7) // 8 * 8
                assert self.off <= SCR * 4, f"scratch overflow {self.off}"
                return o

        def sview(off, n, dt):
            if dt == F32:
                return scr[:, off // 4:off // 4 + n]
            return scrb[:, off // 2:off // 2 + n]

        def sbuf_(name, off, n, dt):
            nb = n * (4 if dt == F32 else 2)
            return P.buf(name, "scr", off, off + nb)

        wstate = dict(next_load=0, next_use=0)

        def w_load(idx):
            slot = idx % 3
            P.dma("pool", f"w{slot}",
                  lambda g, idx=idx, slot=slot: g.dma_start(
                      out=wring[:, slot * KC * 256:(slot + 1) * KC * 256], in_=wstream[idx]),
                  writes=[B_wr[slot]])

        def w_next(depth=2):
            i = wstate["next_use"]
            wstate["next_use"] += 1
            while wstate["next_load"] < min(i + 1 + depth, wstate["limit"]):
                w_load(wstate["next_load"])
                wstate["next_load"] += 1
            return i % 3

        lim = 0
        for l in range(nlayers):
            lim += (14 if l % 2 == 0 else 1)
            if not (stop_after_mixer and l == nlayers - 1):
                lim += NFF
        wstate["limit"] = lim

        pstate = dict(unit=0, bank=0)

        def next_unit_slot():
            s = pstate["unit"] % 4
            pstate["unit"] += 1
            return s

        def next_bank():
            b = pstate["bank"] % 8
            pstate["bank"] += 1
            return b

        def psa(bank, off, n):
            return ps[:, bank * 512 + off:bank * 512 + off + n]

        P.dma("sp", "par", lambda s: s.dma_start(out=par[:], in_=par_d), writes=[B_par])
        for c in range(KC):
            P.dma("sp", "x", lambda s, c=c: s.dma_start(out=xf[:, c * T:(c + 1) * T], in_=xT[c * 128:(c + 1) * 128, :]),
                  writes=B_xf[c])
        P.op("dve", lambda v: v.memset(ones[:], 1.0 / D_MODEL), writes=[B_const])
        P.op("dve", lambda v: v.memset(epsb[:], LN_EPS), writes=[B_const])
        P.op("dve", lambda v: v.memset(nhalf[:], -0.5), writes=[B_const])
        P.op("pool", lambda g: g.iota(invi[:], pattern=[[1, 16]], base=1, channel_multiplier=0), writes=[B_const])
        P.op("dve", lambda v: v.tensor_copy(out=invc[:], in_=invi[:]), reads=[B_const], writes=[B_const])
        P.op("dve", lambda v: v.reciprocal(out=invc[:], in_=invc[:]), reads=[B_const], writes=[B_const])
        xt = P.dma_ch["x"]
        for c in range(KC):
            for u in range(3):
                B_xf[c][u].w = Tick("dma:x", xt[1], xt[0], xt[1])
        for c in range(KC):
            for u, (us, un, _) in enumerate(UNITS):
                eng = "dve" if (c + u) % 2 == 0 else "act"
                if eng == "dve":
                    P.op("dve", lambda v, c=c, us=us, un=un: v.tensor_copy(out=xba(c, us, un), in_=xfa(c, us, un)),
                         reads=[B_xf[c][u]], writes=[B_xb[c][u]])
                else:
                    P.op("act", lambda a, c=c, us=us, un=un: a.activation(out=xba(c, us, un), in_=xfa(c, us, un), func=AF.Copy),
                         reads=[B_xf[c][u]], writes=[B_xb[c][u]])

        LN_BASE = 47104

        def ln_bufs():
            o_sq = LN_BASE
            o_mean = o_sq + KC * 1024 * 2
            o_vr = o_mean + T * 4
            assert o_vr + T * 4 <= SCR * 4
            d = dict(
                B_sq=[sbuf_(f"sq{c}", o_sq + c * 1024 * 2, 1024, BF16) for c in range(KC)],
                B_mean=[sbuf_(f"mean{u}", o_mean + UNITS[u][0] * 4, UNITS[u][1], F32) for u in range(3)],
                B_vr=[sbuf_(f"vr{u}", o_vr + UNITS[u][0] * 4, UNITS[u][1], F32) for u in range(3)],
                sq=lambda c, n: sview(o_sq + c * 1024 * 2, n, BF16),
                mean=lambda s_, n: sview(o_mean + s_ * 4, n, F32),
                vr=lambda s_, n: sview(o_vr + s_ * 4, n, F32))
            return d

        def ln_front_a(L, u):
            us, un, tl = UNITS[u]
            sq = L["sq"]
            for c in range(KC):
                if c % 2 == 0:
                    P.op("dve", lambda v, c=c: v.tensor_copy(out=xba(c, us, un), in_=xfa(c, us, un)),
                         reads=[B_xf[c][u]], writes=[B_xb[c][u]])
                else:
                    P.op("act", lambda a, c=c: a.activation(out=xba(c, us, un), in_=xfa(c, us, un), func=AF.Copy),
                         reads=[B_xf[c][u]], writes=[B_xb[c][u]])
            for c in range(KC):
                P.op("act", lambda a, c=c: a.activation(out=sq(c, un), in_=xfa(c, us, un), func=AF.Square),
                     reads=[B_xf[c][u]], writes=[L["B_sq"][c]])

        def ln_front_pe(L, u):
            us, un, tl = UNITS[u]
            sq = L["sq"]
            sm, sq_slot = next_unit_slot(), next_unit_slot()
            for which, slot in ((0, sm), (1, sq_slot)):
                for bi, ti in enumerate(tl):
                    ts, tn = TILES[ti]
                    for c in range(KC):
                        rhs = xba(c, ts, tn) if which == 0 else sq(c, un)[:, ts - us:ts - us + tn]
                        rb = B_xb[c][u] if which == 0 else L["B_sq"][c]
                        P.op("pe", lambda t, slot=slot, bi=bi, tn=tn, rhs=rhs, c=c: t.matmul(
                            psa(2 * slot + bi, 0, tn), ones[:], rhs, start=(c == 0), stop=(c == KC - 1)),
                            reads=[rb, B_const], writes=[B_bank[2 * slot + bi]], inc=(c == KC - 1))
            return sm, sq_slot

        def ln_front_chain(L, u, slots):
            us, un, tl = UNITS[u]
            mean, vr = L["mean"], L["vr"]
            sm, sq_slot = slots
            bm = [B_bank[2 * sm + bi] for bi in range(len(tl))]
            bq = [B_bank[2 * sq_slot + bi] for bi in range(len(tl))]
            pm = ps[:, sm * 1024:sm * 1024 + un]
            pq = ps[:, sq_slot * 1024:sq_slot * 1024 + un]
            P.op("act", lambda a: a.activation(out=mean(us, un), in_=pm, func=AF.Copy), reads=bm, writes=[L["B_mean"][u]])
            P.op("act", lambda a: a.activation(out=vr(us, un), in_=pm, func=AF.Square), reads=bm, writes=[L["B_vr"][u]])
            P.op("dve", lambda v: v.scalar_tensor_tensor(out=vr(us, un), in0=pq, scalar=LN_EPS, in1=vr(us, un),
                                                         op0=ALU.add, op1=ALU.subtract),
                 reads=bq + [L["B_vr"][u]], writes=[L["B_vr"][u]])
            P.op("act", lambda a: a.activation(out=vr(us, un), in_=vr(us, un), func=AF.Ln),
                 reads=[L["B_vr"][u]], writes=[L["B_vr"][u]])
            P.op("act", lambda a: a.activation(out=vr(us, un), in_=vr(us, un), func=AF.Exp, scale=-0.5),
                 reads=[L["B_vr"][u]], writes=[L["B_vr"][u]])

        def ln_front(L, u):
            ln_front_a(L, u)
            ln_front_chain(L, u, ln_front_pe(L, u))

        def ln_back(L, l, u, goff, boff, make_xb=True):
            us, un, tl = UNITS[u]
            mean, vr = L["mean"], L["vr"]
            for c in range(KC):
                eng = "pool" if (c >= 6 and un >= 512) else "dve"
                P.op(eng, lambda v, c=c: v.tensor_tensor(
                    out=xfa(c, us, un), in0=xfa(c, us, un), in1=mean(us, un), op=ALU.subtract),
                    reads=[B_xf[c][u], L["B_mean"][u]], writes=[B_xf[c][u]])
                P.op(eng, lambda v, c=c: v.tensor_tensor(
                    out=xfa(c, us, un), in0=xfa(c, us, un), in1=vr(us, un), op=ALU.mult),
                    reads=[B_xf[c][u], L["B_vr"][u]], writes=[B_xf[c][u]])
                if make_xb:
                    P.op("act", lambda a, c=c: a.activation(
                        out=xba(c, us, un), in_=xfa(c, us, un), func=AF.Identity, bias=para(l, boff + c),
                        scale=para(l, goff + c)),
                        reads=[B_xf[c][u], B_par], writes=[B_xb[c][u]])
                P.op("act", lambda a, c=c: a.activation(
                    out=xfa(c, us, un), in_=xfa(c, us, un), func=AF.Identity, bias=para(l, boff + c),
                    scale=para(l, goff + c)),
                    reads=[B_xf[c][u], B_par], writes=[B_xf[c][u]])

        def layer_norm(l, goff, boff, make_xb=True, L=None, fronts_done=()):
            if L is None:
                P.gc()
                L = ln_bufs()
            todo = [u for u in range(3) if u not in fronts_done]
            pend = None
            for u in todo:
                ln_front_a(L, u)
                if pend is not None:
                    ln_front_chain(L, *pend)
                pend = (u, ln_front_pe(L, u))
            if pend is not None:
                ln_front_chain(L, *pend)
            for u in range(3):
                ln_back(L, l, u, goff, boff, make_xb)

        def proj_unit(slot_w, col0, u, slot_p, rhs_fn, rhs_bufs, nk=KC, kfn=None):
            us, un, tl = UNITS[u]
            for bi, ti in enumerate(tl):
                ts, tn = TILES[ti]
                for k in range(nk):
                    kk = k if kfn is None else kfn(k)
                    P.op("pe", lambda t, bi=bi, tn=tn, ts=ts, k=k, kk=kk: t.matmul(
                        psa(2 * slot_p + bi, 0, tn), wra(slot_w, kk, col0, 128), rhs_fn(k, ts, tn),
                        start=(k == 0), stop=(k == nk - 1)),
                        reads=[B_wr[slot_w], rhs_bufs(k, u, ti)], writes=[B_bank[2 * slot_p + bi]],
                        inc=(k == nk - 1))
            return [B_bank[2 * slot_p + bi] for bi in range(len(tl))]

        def ffn(l):
            P.gc()
            ar = Arena()
            o_a = ar.take(NA * T * 2)
            o_wd = ar.take(6 * D_MODEL * 2)
            o_h = [ar.take(2 * 1026 * 4) for _ in range(2)]
            o_c = [ar.take(2 * 1024 * 4) for _ in range(3)]
            B_a = [[sbuf_(f"a{s}_{u}", o_a + (s * T + UNITS[u][0]) * 2, UNITS[u][1], BF16) for u in range(3)] for s in range(NA)]
            B_wd = sbuf_("wd", o_wd, 6 * D_MODEL, BF16)
            B_h = [[sbuf_(f"h{q}_{gv}", o_h[q] + gv * 1026 * 4, 1026, F32) for gv in range(2)] for q in range(2)]
            B_c = [[sbuf_(f"c{q}_{gv}", o_c[q] + gv * 1024 * 4, 1024, F32) for gv in range(2)] for q in range(3)]
            aview = lambda s, c0, n: sview(o_a + (s * T + c0) * 2, n, BF16)
            wdv = lambda jj, c0, n: sview(o_wd + (jj * D_MODEL + c0) * 2, n, BF16)
            hv = lambda q, gv, c0, n: sview(o_h[q] + (gv * 1026 + c0) * 4, n, F32)
            cv = lambda q, gv, c0, n: sview(o_c[q] + (gv * 1024 + c0) * 4, n, F32)

            def load_wd(s):
                P.dma("pool", "wd", lambda g, s=s: g.dma_start(out=sview(o_wd, 6 * D_MODEL, BF16), in_=wdown_d[4 * l + s]),
                      writes=[B_wd])

            jobs = []
            st = dict(j=0, prev=None)

            def u_front(m, slot_w, u):
                us, un, tl = UNITS[u]
                q = st["j"] % 2
                qc = st["j"] % 3
                st["j"] += 1
                sg, sv_ = next_unit_slot(), next_unit_slot()
                bg = proj_unit(slot_w, 0, u, sg, lambda k, ts, tn: xba(k, ts, tn), lambda k, u, ti: B_xb[k][u])
                bv = proj_unit(slot_w, 128, u, sv_, lambda k, ts, tn: xba(k, ts, tn), lambda k, u, ti: B_xb[k][u])
                pg = ps[:, sg * 1024:sg * 1024 + un]
                pv = ps[:, sv_ * 1024:sv_ * 1024 + un]
                if u == 0:
                    for gv in range(2):
                        P.op("dve", lambda v, q=q, gv=gv: v.memset(hv(q, gv, 0, 2), 0.0), writes=[B_h[q][gv]])
                else:
                    pun = UNITS[u - 1][1]
                    for gv in range(2):
                        P.op("act", lambda a, q=q, gv=gv, pun=pun: a.activation(
                            out=hv(q, gv, 0, 2), in_=hv(q ^ 1, gv, pun, 2), func=AF.Copy),
                            reads=[B_h[q ^ 1][gv]], writes=[B_h[q][gv]])
                P.op("act", lambda a, q=q, pg=pg, un=un, m=m: a.activation(
                    out=hv(q, 0, 2, un), in_=pg, func=AF.Identity, bias=para(l, PL_BUP + m), scale=1.0),
                    reads=bg + [B_par], writes=[B_h[q][0]])
                P.op("act", lambda a, q=q, pv=pv, un=un, m=m: a.activation(
                    out=hv(q, 1, 2, un), in_=pv, func=AF.Identity, bias=para(l, PL_BUP + NFF + m), scale=1.0),
                    reads=bv + [B_par], writes=[B_h[q][1]])
                for gv in range(2):
                    ch = m + gv * NFF
                    P.op("act", lambda a, q=q, qc=qc, gv=gv, un=un, ch=ch: a.activation(
                        out=cv(qc, gv, 0, un), in_=hv(q, gv, 0, un), func=AF.Identity,
                        bias=para(l, PL_CB + ch), scale=para(l, PL_CW + ch)),
                        reads=[B_h[q][gv], B_par], writes=[B_c[qc][gv]])
                for gv in range(2):
                    ch = m + gv * NFF
                    for k in (1, 2):
                        P.op("dve", lambda v, q=q, qc=qc, gv=gv, un=un, ch=ch, k=k: v.scalar_tensor_tensor(
                            out=cv(qc, gv, 0, un), in0=hv(q, gv, k, un), scalar=para(l, PL_CW + 44 * k + ch),
                            in1=cv(qc, gv, 0, un), op0=ALU.mult, op1=ALU.add),
                            reads=[B_h[q][gv], B_c[qc][gv], B_par], writes=[B_c[qc][gv]])
                return (m, u, qc)

            def u_back(job):
                m, u, q = job
                us, un, tl = UNITS[u]
                aslot = m % NA
                P.op("act", lambda a, q=q, un=un: a.activation(out=cv(q, 0, 0, un), in_=cv(q, 0, 0, un), func=AF.Gelu_apprx_tanh),
                     reads=[B_c[q][0]], writes=[B_c[q][0]])
                P.op("dve", lambda v, q=q, un=un, us=us, aslot=aslot: v.tensor_tensor(
                    out=aview(aslot, us, un), in0=cv(q, 0, 0, un), in1=cv(q, 1, 0, un), op=ALU.mult),
                    reads=[B_c[q][0], B_c[q][1]], writes=[B_a[aslot][u]])

            def u_pair(m):
                slot_w = w_next()
                for u in range(3):
                    job = u_front(m, slot_w, u)
                    if st["prev"] is not None:
                        u_back(st["prev"])
                    st["prev"] = job

            def flush():
                if st["prev"] is not None:
                    u_back(st["prev"])
                    st["prev"] = None

            def d_slab(s, L=None):
                j0, nj = SLABS[s]
                for u, (us, un, tl) in enumerate(UNITS):
                    if L is not None and u > 0:
                        ln_front(L, u - 1)
                    for o in range(KC):
                        slot = next_unit_slot()
                        for bi, ti in enumerate(tl):
                            ts, tn = TILES[ti]
                            for jj in range(nj):
                                aslot = (j0 + jj) % NA
                                P.op("pe", lambda t, slot=slot, bi=bi, tn=tn, ts=ts, jj=jj, o=o, aslot=aslot: t.matmul(
                                    psa(2 * slot + bi, 0, tn), wdv(jj, o * 128, 128), aview(aslot, ts, tn),
                                    start=(jj == 0), stop=(jj == nj - 1)),
                                    reads=[B_wd, B_a[aslot][u]], writes=[B_bank[2 * slot + bi]], inc=(jj == nj - 1))
                        bb = [B_bank[2 * slot + bi] for bi in range(len(tl))]
                        pp = ps[:, slot * 1024:slot * 1024 + un]
                        if s == 0:
                            P.op("dve", lambda v, o=o, us=us, un=un, pp=pp: v.scalar_tensor_tensor(
                                out=xfa(o, us, un), in0=xfa(o, us, un), scalar=ALPHA, in1=pp, op0=ALU.mult, op1=ALU.add),
                                reads=bb + [B_xf[o][u]], writes=[B_xf[o][u]])
                        else:
                            P.op("dve", lambda v, o=o, us=us, un=un, pp=pp: v.tensor_tensor(
                                out=xfa(o, us, un), in0=pp, in1=xfa(o, us, un), op=ALU.add),
                                reads=bb + [B_xf[o][u]], writes=[B_xf[o][u]])

            load_wd(0)
            for s, (j0, nj) in enumerate(SLABS):
                for jj in range(nj):
                    u_pair(j0 + jj)
                    if s > 0 and jj == 1:
                        d_slab(s - 1)
                        load_wd(s)
            flush()
            L = ln_bufs()
            d_slab(len(SLABS) - 1, L)
            ln_front(L, 2)
            return L

        def mixer_even(l):
            i = l // 2
            P.gc()
            ar = Arena()
            o_cat = ar.take(KC * T * 2)
            o_vln = ar.take(NBLK * 512 * 2)
            o_wm = ar.take(4 * 128 * 2)
            o_bs = ar.take(2048 * 4)
            o_lg = ar.take(512 * 4)
            o_lb = ar.take(512 * 4)
            o_st = ar.take(NBLK * 4 * 6 * 4)
            o_mv = ar.take(NBLK * 4 * 2 * 4)
            o_sd = ar.take(NBLK * 4 * 4)
            o_tmp = [ar.take(514 * 4) for _ in range(6)]
            B_cat = [[sbuf_(f"cat{c}_{t}", o_cat + (c * T + TILES[t][0]) * 2, TILES[t][1], BF16) for t in range(5)] for c in range(KC)]
            B_vg = [sbuf_(f"vg{b}", o_cat + b * 512 * 4, 512, F32) for b in range(NBLK)]
            B_vln = [sbuf_(f"vln{b}", o_vln + b * 512 * 2, 512, BF16) for b in range(NBLK)]
            B_gm = sbuf_("gmw", o_wm, (o_lb + 2048 - o_wm) // 2, BF16)
            B_st = sbuf_("st", o_st, (o_sd + NBLK * 16 - o_st) // 4, F32)
            B_tmp = [sbuf_(f"tmp{k}", o_tmp[k], 514, F32) for k in range(6)]
            cat = lambda c, s, n: sview(o_cat + (c * T + s) * 2, n, BF16)
            vg = lambda b, c0, n: sview(o_cat + (b * 512 + c0) * 4, n, F32)
            vln = lambda b, c0, n: sview(o_vln + (b * 512 + c0) * 2, n, BF16)
            wm = lambda h: sview(o_wm + h * 128 * 2, 128, BF16)
            bsv = lambda h, n: sview(o_bs + h * 512 * 4, n, F32)
            tmp = lambda k, c0, n: sview(o_tmp[k] + c0 * 4, n, F32)
            stv = lambda b, h: sview(o_st + (b * 4 + h) * 6 * 4, 6, F32)
            mvv = lambda b, h, j: sview(o_mv + ((b * 4 + h) * 2 + j) * 4, 1, F32)

            P.op("dve", lambda v: v.memset(sview(o_wm, 512, BF16), 0.0), writes=[B_gm])
            wmt = sview(o_wm, 512, BF16)
            P.dma("pool", "gm", lambda g: g.dma_start(out=wmt[0:64, :], in_=gws_d[i, 0:64].rearrange("p h i -> p (h i)")),
                  writes=[B_gm])
            for h in range(4):
                P.dma("pool", "gm", lambda g, h=h: g.dma_start(out=wmt[64:128, h * 128 + 64:h * 128 + 128],
                                                                 in_=gws_d[i, 64:128, h, 64:128]), writes=[B_gm])
            P.dma("sp", "gm", lambda s: s.dma_start(out=sview(o_bs, 2048, F32), in_=gbs_d[i]), writes=[B_gm])
            P.dma("sp", "gm", lambda s: s.dma_start(out=sview(o_lg, 512, F32), in_=glg_d[i]), writes=[B_gm])
            P.dma("sp", "gm", lambda s: s.dma_start(out=sview(o_lb, 512, F32), in_=glb_d[i]), writes=[B_gm])
            gt = P.dma_ch["gm"]
            B_gm.w = Tick("dma:gm", gt[1], gt[0], gt[1])

            sv0, sv1 = w_next(1), w_next(1)
            for b in range(NBLK):
                u = unit_of_tile(b // 4)
                bank = next_bank()
                for it, slot_w in enumerate((sv0, sv1)):
                    for k in range(KC):
                        P.op("pe", lambda t, bank=bank, it=it, slot_w=slot_w, k=k, b=b: t.matmul(
                            psa(bank, it * 256, 256), xba(k, b * 128, 128), wra(slot_w, k, 0, 256),
                            start=(k == 0), stop=(k == KC - 1)),
                            reads=[B_wr[slot_w], B_xb[k][u]], writes=[B_bank[bank]], inc=(k == KC - 1))
                P.op("act", lambda a, bank=bank, b=b: a.activation(out=vg(b, 0, 512), in_=psa(bank, 0, 512), func=AF.Gelu_apprx_tanh),
                     reads=[B_bank[bank]], writes=[B_vg[b]])
                for h in range(4):
                    P.op("dve", lambda v, b=b, h=h: v.bn_stats(out=stv(b, h), in_=vg(b, h * 128, 128)),
                         reads=[B_vg[b]], writes=[B_st])
                for h in range(4):
                    P.op("dve", lambda v, b=b, h=h: v.bn_aggr(out=sview(o_mv + (b * 4 + h) * 8, 2, F32), in_=stv(b, h)),
                         reads=[B_st], writes=[B_st])
            var_all = scr[:, o_mv // 4 + 1:o_mv // 4 + 1 + 2 * NBLK * 4:2]
            P.op("act", lambda a: a.activation(out=sview(o_sd, NBLK * 4, F32), in_=var_all, func=AF.Ln, bias=epsb[:, 0:1], scale=1.0),
                 reads=[B_st, B_const], writes=[B_st])
            P.op("act", lambda a: a.activation(out=sview(o_sd, NBLK * 4, F32), in_=sview(o_sd, NBLK * 4, F32), func=AF.Exp, scale=-0.5),
                 reads=[B_st], writes=[B_st])
            for b in range(NBLK):
                for h in range(4):
                    P.op("dve", lambda v, b=b, h=h: v.tensor_scalar(
                        out=vg(b, h * 128, 128), in0=vg(b, h * 128, 128), scalar1=mvv(b, h, 0),
                        scalar2=sview(o_sd + (b * 4 + h) * 4, 1, F32), op0=ALU.subtract, op1=ALU.mult),
                        reads=[B_vg[b], B_st], writes=[B_vg[b]])
                P.op("dve", lambda v, b=b: v.tensor_tensor(out=vg(b, 0, 512), in0=vg(b, 0, 512), in1=sview(o_lg, 512, F32), op=ALU.mult),
                     reads=[B_vg[b], B_gm], writes=[B_vg[b]])
                P.op("dve", lambda v, b=b: v.tensor_tensor(out=vln(b, 0, 512), in0=vg(b, 0, 512), in1=sview(o_lb, 512, F32), op=ALU.add),
                     reads=[B_vg[b], B_gm], writes=[B_vln[b]])

            if debug == "vln" and l == 0:
                P.dma("sp", "out", lambda s_: s_.dma_start(out=dbgb[:, 0:NBLK * 512], in_=sview(o_vln, NBLK * 512, BF16)),
                      reads=B_vln)
            tq = dict(q=0)
            for hp in range(2):
                slot_w = w_next()
                for hh in range(2):
                    h = 2 * hp + hh
                    for ti, (ts, tn) in enumerate(TILES):
                        u = unit_of_tile(ti)
                        q = tq["q"]
                        tq["q"] ^= 1
                        bu, bs_ = next_bank(), next_bank()
                        for k in range(KC):
                            P.op("pe", lambda t, bu=bu, k=k, ts=ts, tn=tn, hh=hh, slot_w=slot_w: t.matmul(
                                psa(bu, 0, tn), wra(slot_w, k, hh * 128, 128), xba(k, ts, tn),
                                start=(k == 0), stop=(k == KC - 1)),
                                reads=[B_wr[slot_w], B_xb[k][u]], writes=[B_bank[bu]], inc=(k == KC - 1))
                        for bi in range(tn // 128):
                            b = ts // 128 + bi
                            P.op("pe", lambda t, bs_=bs_, bi=bi, b=b, h=h: t.matmul(
                                psa(bs_, bi * 128, 128), vln(b, h * 128, 128), wm(h), start=True, stop=True),
                                reads=[B_vln[b], B_gm], writes=[B_bank[bs_]], inc=(bi == tn // 128 - 1))
                        P.op("act", lambda a, q=q, bu=bu, tn=tn: a.activation(out=tmp(q, 0, tn), in_=psa(bu, 0, tn), func=AF.Gelu_apprx_tanh),
                             reads=[B_bank[bu]], writes=[B_tmp[q]])
                        P.op("dve", lambda v, q=q, bs_=bs_, tn=tn, h=h: v.tensor_tensor(
                            out=tmp(2 + q, 0, tn), in0=psa(bs_, 0, tn), in1=bsv(h, tn), op=ALU.add),
                            reads=[B_bank[bs_], B_gm], writes=[B_tmp[2 + q]])
                        P.op("dve", lambda v, q=q, tn=tn, ts=ts, h=h: v.tensor_tensor(
                            out=cat(h, ts, tn), in0=tmp(2 + q, 0, tn), in1=tmp(q, 0, tn), op=ALU.mult),
                            reads=[B_tmp[q], B_tmp[2 + q]], writes=[B_cat[h][ti]])

            cur = dict(slot=None, pos=2)

            def next_chunk():
                if cur["pos"] == 2:
                    cur["slot"] = w_next(1)
                    cur["pos"] = 0
                r = (cur["slot"], cur["pos"] * 128)
                cur["pos"] += 1
                return r

            sq_ = dict(q=0)
            for j in range(4):
                wh, wgc, wgb = next_chunk(), next_chunk(), next_chunk()
                for ti, (ts, tn) in enumerate(TILES):
                    u = unit_of_tile(ti)
                    q = sq_["q"]
                    sq_["q"] ^= 1
                    banks = []
                    for (slot_w, c0) in (wh, wgc, wgb):
                        bk = next_bank()
                        banks.append(bk)
                        for k in range(KC):
                            P.op("pe", lambda t, bk=bk, k=k, ts=ts, tn=tn, slot_w=slot_w, c0=c0: t.matmul(
                                psa(bk, 0, tn), wra(slot_w, k, c0, 128), xba(k, ts, tn),
                                start=(k == 0), stop=(k == KC - 1)),
                                reads=[B_wr[slot_w], B_xb[k][u]], writes=[B_bank[bk]], inc=(k == KC - 1))
                    bh, bgc, bgb = banks
                    P.op("act", lambda a, q=q, bh=bh, tn=tn: a.activation(out=tmp(q, 0, tn), in_=psa(bh, 0, tn), func=AF.Copy),
                         reads=[B_bank[bh]], writes=[B_tmp[q]])
                    if ti == 0:
                        P.op("dve", lambda v, q=q: v.memset(tmp(2 + q, 0, 2), 0.0), writes=[B_tmp[2 + q]])
                    else:
                        ptn = TILES[ti - 1][1]
                        P.op("act", lambda a, q=q, ptn=ptn: a.activation(out=tmp(2 + q, 0, 2), in_=tmp(2 + (q ^ 1), ptn, 2), func=AF.Copy),
                             reads=[B_tmp[2 + (q ^ 1)]], writes=[B_tmp[2 + q]])
                    P.op("dve", lambda v, q=q, bgc=bgc, tn=tn: v.tensor_tensor(
                        out=tmp(2 + q, 2, tn), in0=psa(bgc, 0, tn), in1=tmp(q, 0, tn), op=ALU.mult),
                        reads=[B_bank[bgc], B_tmp[q]], writes=[B_tmp[2 + q]])
                    P.op("dve", lambda v, q=q, tn=tn, j=j: v.tensor_scalar(
                        out=tmp(4 + q, 0, tn), in0=tmp(2 + q, 0, tn), scalar1=para(l, PL_SW + j), scalar2=None, op0=ALU.mult),
                        reads=[B_tmp[2 + q], B_par], writes=[B_tmp[4 + q]])
                    for k in (1, 2):
                        P.op("dve", lambda v, q=q, tn=tn, j=j, k=k: v.scalar_tensor_tensor(
                            out=tmp(4 + q, 0, tn), in0=tmp(2 + q, k, tn), scalar=para(l, PL_SW + 4 * k + j),
                            in1=tmp(4 + q, 0, tn), op0=ALU.mult, op1=ALU.add),
                            reads=[B_tmp[2 + q], B_tmp[4 + q], B_par], writes=[B_tmp[4 + q]])
                    P.op("dve", lambda v, q=q, bgb=bgb, tn=tn, ts=ts, j=j: v.tensor_tensor(
                        out=cat(4 + j, ts, tn), in0=psa(bgb, 0, tn), in1=tmp(4 + q, 0, tn), op=ALU.mult),
                        reads=[B_bank[bgb], B_tmp[4 + q]], writes=[B_cat[4 + j][ti]])

            if debug == "cat" and l == 0:
                P.dma("sp", "out", lambda s_: s_.dma_start(out=dbgb, in_=sview(o_cat, KC * T, BF16)),
                      reads=[b_ for row in B_cat for b_ in row])
            for it in range(4):
                slot_w = w_next()
                for mo in range(2):
                    o = 2 * it + mo
                    for u, (us, un, tl) in enumerate(UNITS):
                        slot = next_unit_slot()
                        bb = proj_unit(slot_w, mo * 128, u, slot, lambda k, ts, tn: cat(k, ts, tn),
                                       lambda k, u, ti: B_cat[k][ti])
                        pp = ps[:, slot * 1024:slot * 1024 + un]
                        P.op("dve", lambda v, o=o, us=us, un=un, pp=pp: v.scalar_tensor_tensor(
                            out=xfa(o, us, un), in0=xfa(o, us, un), scalar=ALPHA, in1=pp, op0=ALU.mult, op1=ALU.add),
                            reads=bb + [B_xf[o][u]], writes=[B_xf[o][u]])

        def mixer_odd(l):
            i = l // 2
            P.gc()
            ar = Arena()
            SP = 16
            o_p = ar.take(KC * T * 2)
            o_s = [ar.take((T + SP) * 4) for _ in range(4)]
            o_t = [ar.take(16 * 4) for _ in range(2)]
            B_p = [[sbuf_(f"p{c}_{u}", o_p + (c * T + UNITS[u][0]) * 2, UNITS[u][1], BF16) for u in range(3)] for c in range(KC)]
            B_s = [sbuf_(f"S{k}", o_s[k], T + SP, F32) for k in range(4)]
            B_t = [sbuf_(f"ptmp{k}", o_t[k], 16, F32) for k in range(2)]
            pv = lambda c, s_, n: sview(o_p + (c * T + s_) * 2, n, BF16)
            sv = lambda k, s_, n: sview(o_s[k] + (SP + s_) * 4, n, F32)
            slot_w = w_next()
            for k in range(4):
                P.op("dve", lambda v, k=k: v.memset(sview(o_s[k], SP, F32), 0.0), writes=[B_s[k]])
            L = ln_bufs()
            for c in (4, 5, 0, 1, 2, 3, 6, 7):
                g = c // 2
                w = C_WINDOWS[g]
                seng = "pool" if g == 2 else "dve"
                sb_ = 2 if g == 2 else 0
                tb = 1 if g == 2 else 0
                P.op(seng, lambda v, c=c, sb_=sb_: v.tensor_tensor(
                    out=sv(sb_, 1, T - 1), in0=xfa(c, 1, T - 1), in1=xfa(c, 0, T - 1), op=ALU.add),
                    reads=list(B_xf[c]), writes=[B_s[sb_]])
                P.op("act", lambda a, c=c, sb_=sb_: a.activation(out=sv(sb_, 0, 1), in_=xfa(c, 0, 1), func=AF.Copy),
                     reads=[B_xf[c][0]], writes=[B_s[sb_]])
                src, sh = sb_, 2
                while sh < w:
                    dst = sb_ + (1 - (src - sb_))
                    P.op(seng, lambda v, src=src, dst=dst, sh=sh: v.tensor_tensor(
                        out=sv(dst, 0, T), in0=sv(src, 0, T), in1=sv(src, -sh, T), op=ALU.add),
                        reads=[B_s[src]], writes=[B_s[dst]])
                    src = dst
                    sh *= 2
                P.op("dve", lambda v, c=c, src=src, w=w: v.scalar_tensor_tensor(
                    out=pv(c, 0, T), in0=sv(src, 0, T), scalar=1.0 / w, in1=xfa(c, 0, T), op0=ALU.mult, op1=ALU.subtract),
                    reads=[B_s[src]] + list(B_xf[c]), writes=list(B_p[c]))
                P.op("dve", lambda v, src=src, w=w, tb=tb: v.tensor_tensor(
                    out=sview(o_t[tb], w - 1, F32), in0=sv(src, 0, w - 1), in1=invc[:, 0:w - 1], op=ALU.mult),
                    reads=[B_s[src], B_const], writes=[B_t[tb]])
                P.op("dve", lambda v, c=c, w=w, tb=tb: v.tensor_tensor(
                    out=pv(c, 0, w - 1), in0=sview(o_t[tb], w - 1, F32), in1=xfa(c, 0, w - 1), op=ALU.subtract),
                    reads=[B_t[tb], B_xf[c][0]], writes=[B_p[c][0]])
            for u, (us, un, tl) in enumerate(UNITS):
                for o in range(KC):
                    g, mo = o // 2, o % 2
                    slot = next_unit_slot()
                    bb = proj_unit(slot_w, mo * 128, u, slot, lambda k, ts, tn, g=g: pv(2 * g + k, ts, tn),
                                   lambda k, u, ti, g=g: B_p[2 * g + k][u], nk=2, kfn=lambda k, g=g: 2 * g + k)
                    pp = ps[:, slot * 1024:slot * 1024 + un]
                    P.op("act", lambda a, o=o, us=us, un=un: a.activation(
                        out=xfa(o, us, un), in_=xfa(o, us, un), func=AF.Copy, scale=ALPHA),
                        reads=[B_xf[o][u]], writes=[B_xf[o][u]])
                    P.op("dve", lambda v, o=o, us=us, un=un, pp=pp: v.scalar_tensor_tensor(
                        out=xfa(o, us, un), in0=pp, scalar=para(l, PL_PS + o), in1=xfa(o, us, un), op0=ALU.mult, op1=ALU.add),
                        reads=bb + [B_xf[o][u], B_par], writes=[B_xf[o][u]])
                ln_front(L, u)
            return L

        for l in range(nlayers):
            last = (l == nlayers - 1)
            if l % 2 == 0:
                mixer_even(l)
                Lm, fd = None, ()
            else:
                Lm, fd = mixer_odd(l), (0, 1, 2)
            layer_norm(l, PL_LNMG, PL_LNMB, make_xb=not (last and stop_after_mixer), L=Lm, fronts_done=fd)
            if last and stop_after_mixer:
                break
            L = ffn(l)
            layer_norm(l, PL_LNFG, PL_LNFB, make_xb=not last, L=L, fronts_done=(0, 1, 2))

        for u, (us, un, tl) in enumerate(UNITS):
            for c in range(KC):
                P.dma("sp", "out", lambda s, c=c, us=us, un=un: s.dma_start(
                    out=outT[c * 128:(c + 1) * 128, us:us + un], in_=xfa(c, us, un)), reads=[B_xf[c][u]])
        ot = P.dma_ch["out"]
        P.final_wait("sp", [Tick("dma:out", ot[1], ot[0], ot[1])])

        with nc.Block() as block:
            @block.tensor
            def _(h):
                P.replay("pe", h)

            @block.scalar
            def _(h):
                P.replay("act", h)

            @block.vector
            def _(h):
                P.replay("dve", h)

            @block.gpsimd
            def _(h):
                P.replay("pool", h)

            @block.sync
            def _(h):
                P.replay("sp", h)
    return nc


def core_slices():
    sl = []
    for b in range(BATCH):
        sl.append((b, 0, T))
        sl.append((b, SEQ - T, SEQ))
    return sl


def kernel(**inputs):
    x = np.asarray(inputs["x"], np.float32)
    w = prep_weights(inputs)
    in_maps = []
    for (b, s, e) in core_slices():
        m = dict(w)
        m["xT"] = np.ascontiguousarray(x[b, s:e, :].T)
        in_maps.append(m)
    nc = build_program()
    res = run_bass_kernel_spmd(nc, in_maps, core_ids=list(range(8)))
    out = np.empty((BATCH, SEQ, D_MODEL), np.float32)
    for ci, (b, s, e) in enumerate(core_slices()):
        oT = res.results[ci]["outT"]
        if s == 0:
            out[b, 0:OWN_A, :] = oT[:, 0:OWN_A].T
        else:
            out[b, OWN_A:SEQ, :] = oT[:, OWN_A - s:T].T
    return out
```

```python
import contextlib
import numpy as np
import concourse.bass as bass
import concourse.mybir as mybir
from concourse.bass_utils import run_bass_kernel_spmd

F32 = mybir.dt.float32
BF16 = mybir.dt.bfloat16
I32 = mybir.dt.int32
AF = mybir.ActivationFunctionType
ALU = mybir.AluOpType

D_MODEL = 1024
SEQ = 4096
BATCH = 4
DEPTH = 4
KC = 8
D_FF = 2816
NFF = 22
ALPHA = (2.0 * DEPTH) ** 0.25
LN_EPS = 1e-5
T = 2176
OWN_A = 2088
NBLK = T // 128
TILES = [(0, 512), (512, 512), (1024, 512), (1536, 512), (2048, 128)]
UNITS = [(0, 1024, [0, 1]), (1024, 1024, [2, 3]), (2048, 128, [4])]
SLABS = [(0, 6), (6, 6), (12, 5), (17, 5)]
NA = 8
C_WINDOWS = (2, 4, 8, 16)

PL_LNMG, PL_LNMB, PL_LNFG, PL_LNFB = 0, 8, 16, 24
PL_BUP = 32
PL_CW = 76
PL_CB = 208
PL_PS = 252
PL_SW = 260
PL_N = 272
NPAR = PL_N * DEPTH


class Tick:
    __slots__ = ("eng", "seq", "sem", "val")

    def __init__(self, eng, seq, sem, val):
        self.eng, self.seq, self.sem, self.val = eng, seq, sem, val


class Buf:
    __slots__ = ("name", "arena", "lo", "hi", "w", "r")

    def __init__(self, name, arena=None, lo=0, hi=0):
        self.name, self.arena, self.lo, self.hi = name, arena, lo, hi
        self.w = None
        self.r = {}


class Eng:
    def __init__(self, name, sems, roll=12000):
        self.name = name
        self.sems = sems
        self.roll = roll
        self.seq = 0
        self.ops = []
        self.waited = {}
        self.pending = []

    def next_tick(self):
        self.seq += 1
        si = (self.seq - 1) // self.roll
        t = Tick(self.name, self.seq, self.sems[si], (self.seq - 1) % self.roll + 1)
        for p in self.pending:
            p.seq, p.sem, p.val = t.seq, t.sem, t.val
        self.pending = []
        return t


class Prog:
    def __init__(self):
        self.engs = {}
        self.arena_bufs = {}
        self.dma_ch = {}

    def add_engine(self, name, sems):
        self.engs[name] = Eng(name, sems)

    def add_dma_channel(self, name, sem):
        self.dma_ch[name] = [sem, 0]

    def buf(self, name, arena=None, lo=0, hi=0):
        b = Buf(name, arena, lo, hi)
        if arena is not None:
            self.arena_bufs.setdefault(arena, []).append(b)
        return b

    def _overl(self, b):
        if b.arena is None:
            return ()
        return [o for o in self.arena_bufs[b.arena]
                if o is not b and o.lo < b.hi and b.lo < o.hi and (o.w is not None or o.r)]

    def _deps(self, eng, reads, writes):
        raw, other = [], []
        for b in reads:
            if b.w is not None:
                raw.append(b.w)
            for o in self._overl(b):
                if o.w is not None:
                    raw.append(o.w)
        for b in writes:
            if b.w is not None:
                other.append(b.w)
            other.extend(b.r.values())
            for o in self._overl(b):
                if o.w is not None:
                    other.append(o.w)
                other.extend(o.r.values())
                o.w, o.r = None, {}
        return raw, other

    def _waits(self, e, raw, other):
        waits = []
        for lst, is_raw in ((raw, True), (other, False)):
            for t in lst:
                if t.eng == e.name:
                    if not is_raw or e.name == "pe":
                        continue
                if t.eng.startswith("dma:"):
                    key = t.eng
                    if e.waited.get(key, 0) >= t.val:
                        continue
                    e.waited[key] = t.val
                    waits.append((t.sem, t.val))
                else:
                    assert t.seq is not None, "unresolved deferred tick"
                    if e.waited.get(t.eng, 0) >= t.seq:
                        continue
                    e.waited[t.eng] = t.seq
                    waits.append((t.sem, t.val))
        return waits

    def gc(self):
        for a in self.arena_bufs:
            self.arena_bufs[a] = [b for b in self.arena_bufs[a] if b.w is not None or b.r]

    def op(self, eng, fn, reads=(), writes=(), inc=True):
        e = self.engs[eng]
        raw, other = self._deps(e, reads, writes)
        waits = self._waits(e, raw, other)
        if inc:
            tick = e.next_tick()
            e.ops.append((waits, fn, tick, 1))
        else:
            tick = Tick(eng, None, None, None)
            e.pending.append(tick)
            e.ops.append((waits, fn, None, 0))
        for b in reads:
            b.r[eng] = tick
        for b in writes:
            b.w, b.r = tick, {}
        return tick

    def dma(self, eng, ch, fn, reads=(), writes=()):
        e = self.engs[eng]
        raw, other = self._deps(e, reads, writes)
        waits = self._waits(e, raw, other)
        c = self.dma_ch[ch]
        c[1] += 16
        tick = Tick("dma:" + ch, c[1], c[0], c[1])
        e.ops.append((waits, fn, tick, 16))
        for b in reads:
            b.r["dma:" + ch] = tick
        for b in writes:
            b.w, b.r = tick, {}
        return tick

    def final_wait(self, eng, ticks):
        e = self.engs[eng]
        waits = self._waits(e, list(ticks), [])
        e.ops.append((waits, None, None, 0))

    def replay(self, eng, h):
        for waits, fn, tick, inc in self.engs[eng].ops:
            for sem, val in waits:
                h.wait_ge(sem, val)
            if fn is None:
                continue
            ins = fn(h)
            if tick is not None:
                ins.then_inc(tick.sem, inc)


def _items(W, chunk_order):
    n = len(chunk_order) // 2
    Wr = W.reshape(KC, 128, -1)
    out = np.empty((n, 128, KC, 256), np.float32)
    for i in range(n):
        for j in range(2):
            c = chunk_order[2 * i + j]
            out[i, :, :, j * 128:(j + 1) * 128] = Wr[:, :, c * 128:(c + 1) * 128].transpose(1, 0, 2)
    return out.reshape(n, 128, KC * 256)


def _fm(vec):
    return np.ascontiguousarray(np.asarray(vec, np.float32).reshape(-1, 128).T)


EVEN_IN_ORDER = [4, 5, 6, 7, 0, 1, 2, 3] + [c for j in range(4) for c in (16 + j, 12 + j, 8 + j)]


def prep_weights(inp):
    items, wdown = [], []
    par = np.zeros((128, NPAR), np.float32)
    gm = {}
    for l in range(DEPTH):
        i = l // 2
        if l % 2 == 0:
            items.append(_items(np.asarray(inp["w_in_even"][i]), EVEN_IN_ORDER))
            items.append(_items(np.asarray(inp["w_out_even"][i]), list(range(8))))
        else:
            pw = np.asarray(inp["pool_w"][i], np.float32)
            it = pw.reshape(4, 2, 128, 256).transpose(2, 0, 1, 3).reshape(1, 128, KC * 256)
            items.append(np.ascontiguousarray(it))
        items.append(_items(np.asarray(inp["ffn_w_up"][l]), [c for m in range(NFF) for c in (m, NFF + m)]))
        wd = np.asarray(inp["ffn_w_down"][l], np.float32).reshape(NFF, 128, D_MODEL)
        for (j0, nj) in SLABS:
            s = np.zeros((128, 6, D_MODEL), np.float32)
            s[:, :nj, :] = wd[j0:j0 + nj].transpose(1, 0, 2)
            wdown.append(s.reshape(1, 128, 6 * D_MODEL))
        o = l * PL_N
        par[:, o + PL_LNMG:o + PL_LNMG + 8] = _fm(inp["ln_mix_g"][l])
        par[:, o + PL_LNMB:o + PL_LNMB + 8] = _fm(inp["ln_mix_b"][l])
        par[:, o + PL_LNFG:o + PL_LNFG + 8] = _fm(inp["ln_ffn_g"][l])
        par[:, o + PL_LNFB:o + PL_LNFB + 8] = _fm(inp["ln_ffn_b"][l])
        par[:, o + PL_BUP:o + PL_BUP + 44] = _fm(inp["ffn_b_up"][l])
        cw = np.asarray(inp["ffn_conv_w"][l], np.float32)
        for k in range(3):
            par[:, o + PL_CW + 44 * k:o + PL_CW + 44 * (k + 1)] = _fm(cw[k])
        par[:, o + PL_CB:o + PL_CB + 44] = _fm(inp["ffn_conv_b"][l])
        if l % 2 == 1:
            par[:, o + PL_PS:o + PL_PS + 8] = _fm(inp["pool_scale"][i])
        else:
            sw = np.asarray(inp["sconv_w"][i], np.float32)
            for k in range(3):
                par[:, o + PL_SW + 4 * k:o + PL_SW + 4 * (k + 1)] = _fm(sw[k])
    gws = np.asarray(inp["gmlp_ws"], np.float32)
    gm["gws"] = np.ascontiguousarray(gws.transpose(0, 3, 1, 2))
    gbs = np.asarray(inp["gmlp_bs"], np.float32)
    gm["gbs"] = np.ascontiguousarray(np.broadcast_to(
        np.tile(gbs[:, None, :, None, :], (1, 1, 1, 4, 1)).reshape(2, 1, 4 * 512), (2, 128, 2048)))
    gm["glg"] = np.ascontiguousarray(np.broadcast_to(np.asarray(inp["gmlp_ln_g"], np.float32)[:, None, :], (2, 128, 512)))
    gm["glb"] = np.ascontiguousarray(np.broadcast_to(np.asarray(inp["gmlp_ln_b"], np.float32)[:, None, :], (2, 128, 512)))
    return dict(wstream=np.concatenate(items, 0), wdown=np.concatenate(wdown, 0), par=par, **gm)


def build_program(nlayers=DEPTH, stop_after_mixer=False, debug=None):
    nc = bass.Bass("TRN2", target_bir_lowering=False)
    n_items = sum((14 if l % 2 == 0 else 1) + NFF for l in range(DEPTH))
    xT = nc.dram_tensor("xT", [D_MODEL, T], F32, kind="ExternalInput").ap()
    wstream = nc.dram_tensor("wstream", [n_items, 128, KC * 256], F32, kind="ExternalInput").ap()
    wdown_d = nc.dram_tensor("wdown", [4 * DEPTH, 128, 6 * D_MODEL], F32, kind="ExternalInput").ap()
    par_d = nc.dram_tensor("par", [128, NPAR], F32, kind="ExternalInput").ap()
    gws_d = nc.dram_tensor("gws", [2, 128, 4, 128], F32, kind="ExternalInput").ap()
    gbs_d = nc.dram_tensor("gbs", [2, 128, 2048], F32, kind="ExternalInput").ap()
    glg_d = nc.dram_tensor("glg", [2, 128, 512], F32, kind="ExternalInput").ap()
    glb_d = nc.dram_tensor("glb", [2, 128, 512], F32, kind="ExternalInput").ap()
    outT = nc.dram_tensor("outT", [D_MODEL, T], F32, kind="ExternalOutput").ap()
    if debug:
        dbgb = nc.dram_tensor("dbgb", [128, KC * T], BF16, kind="ExternalOutput").ap()
        dbgf = nc.dram_tensor("dbgf", [128, KC * T], F32, kind="ExternalOutput").ap()

    es = contextlib.ExitStack()
    with es:
        def sb(name, shape, dt):
            return es.enter_context(nc.sbuf_tensor("sb_" + name, shape, dt))

        xf = sb("xf", [128, KC * T], F32)
        xb = sb("xb", [128, KC * T], BF16)
        wring = sb("wring", [128, 3 * KC * 256], BF16)
        par = sb("par", [128, NPAR], F32)
        ones = sb("ones", [128, 128], BF16)
        invc = sb("invc", [128, 16], F32)
        invi = sb("invi", [128, 16], I32)
        epsb = sb("epsb", [128, 1], F32)
        nhalf = sb("nhalf", [128, 1], F32)
        SCR = 22100
        scr = sb("scr", [128, SCR], F32)
        scrb = scr.bitcast(BF16)
        ps = es.enter_context(nc.psum_tensor("ps", [128, 8 * 512], F32))

        P = Prog()
        for en in ("pe", "act", "dve", "pool", "sp"):
            P.add_engine(en, [es.enter_context(nc.semaphore(f"s_{en}{i}")) for i in range(3)])
        for ch in ("par", "x", "out", "w0", "w1", "w2", "wd", "gm"):
            P.add_dma_channel(ch, es.enter_context(nc.semaphore(f"d_{ch}")))

        B_xf = [[P.buf(f"xf{c}_{u}") for u in range(3)] for c in range(KC)]
        B_xb = [[P.buf(f"xb{c}_{u}") for u in range(3)] for c in range(KC)]
        B_bank = [P.buf(f"bank{b}") for b in range(8)]
        B_wr = [P.buf(f"wr{s}") for s in range(3)]
        B_par = P.buf("par")
        B_const = P.buf("const")

        def xfa(c, s, n):
            return xf[:, c * T + s:c * T + s + n]

        def xba(c, s, n):
            return xb[:, c * T + s:c * T + s + n]

        def wra(slot, k, c0, n):
            o = slot * KC * 256 + k * 256 + c0
            return wring[:, o:o + n]

        def para(l, off, n=1):
            o = l * PL_N + off
            return par[:, o:o + n]

        def unit_of_tile(ti):
            return 0 if ti < 2 else (1 if ti < 4 else 2)

        class Arena:
            def __init__(self):
                self.off = 0

            def take(self, nbytes):
                o = self.off
                self.off += (nbytes + 7) // 8 * 8
                assert self.off <= SCR * 4, f"scratch overflow {self.off}"
                return o

        def sview(off, n, dt):
            if dt == F32:
                return scr[:, off // 4:off // 4 + n]
            return scrb[:, off // 2:off // 2 + n]

        def sbuf_(name, off, n, dt):
            nb = n * (4 if dt == F32 else 2)
            return P.buf(name, "scr", off, off + nb)

        wstate = dict(next_load=0, next_use=0)

        def w_load(idx):
            slot = idx % 3
            P.dma("pool", f"w{slot}",
                  lambda g, idx=idx, slot=slot: g.dma_start(
                      out=wring[:, slot * KC * 256:(slot + 1) * KC * 256], in_=wstream[idx]),
                  writes=[B_wr[slot]])

        def w_next(depth=2):
            i = wstate["next_use"]
            wstate["next_use"] += 1
            while wstate["next_load"] < min(i + 1 + depth, wstate["limit"]):
                w_load(wstate["next_load"])
                wstate["next_load"] += 1
            return i % 3

        lim = 0
        for l in range(nlayers):
            lim += (14 if l % 2 == 0 else 1)
            if not (stop_after_mixer and l == nlayers - 1):
                lim += NFF
        wstate["limit"] = lim

        pstate = dict(unit=0, bank=0)

        def next_unit_slot():
            s = pstate["unit"] % 4
            pstate["unit"] += 1
            return s

        def next_bank():
            b = pstate["bank"] % 8
            pstate["bank"] += 1
            return b

        def psa(bank, off, n):
            return ps[:, bank * 512 + off:bank * 512 + off + n]

        P.dma("sp", "par", lambda s: s.dma_start(out=par[:], in_=par_d), writes=[B_par])
        for c in range(KC):
            P.dma("sp", "x", lambda s, c=c: s.dma_start(out=xf[:, c * T:(c + 1) * T], in_=xT[c * 128:(c + 1) * 128, :]),
                  writes=B_xf[c])
        P.op("dve", lambda v: v.memset(ones[:], 1.0 / D_MODEL), writes=[B_const])
        P.op("dve", lambda v: v.memset(epsb[:], LN_EPS), writes=[B_const])
        P.op("dve", lambda v: v.memset(nhalf[:], -0.5), writes=[B_const])
        P.op("pool", lambda g: g.iota(invi[:], pattern=[[1, 16]], base=1, channel_multiplier=0), writes=[B_const])
        P.op("dve", lambda v: v.tensor_copy(out=invc[:], in_=invi[:]), reads=[B_const], writes=[B_const])
        P.op("dve", lambda v: v.reciprocal(out=invc[:], in_=invc[:]), reads=[B_const], writes=[B_const])
        xt = P.dma_ch["x"]
        for c in range(KC):
            for u in range(3):
                B_xf[c][u].w = Tick("dma:x", xt[1], xt[0], xt[1])
        for c in range(KC):
            for u, (us, un, _) in enumerate(UNITS):
                eng = "dve" if (c + u) % 2 == 0 else "act"
                if eng == "dve":
                    P.op("dve", lambda v, c=c, us=us, un=un: v.tensor_copy(out=xba(c, us, un), in_=xfa(c, us, un)),
                         reads=[B_xf[c][u]], writes=[B_xb[c][u]])
                else:
                    P.op("act", lambda a, c=c, us=us, un=un: a.activation(out=xba(c, us, un), in_=xfa(c, us, un), func=AF.Copy),
                         reads=[B_xf[c][u]], writes=[B_xb[c][u]])

        LN_BASE = 47104

        def ln_bufs():
            o_sq = LN_BASE
            o_mean = o_sq + KC * 1024 * 2
            o_vr = o_mean + T * 4
            assert o_vr + T * 4 <= SCR * 4
            d = dict(
                B_sq=[sbuf_(f"sq{c}", o_sq + c * 1024 * 2, 1024, BF16) for c in range(KC)],
                B_mean=[sbuf_(f"mean{u}", o_mean + UNITS[u][0] * 4, UNITS[u][1], F32) for u in range(3)],
                B_vr=[sbuf_(f"vr{u}", o_vr + UNITS[u][0] * 4, UNITS[u][1], F32) for u in range(3)],
                sq=lambda c, n: sview(o_sq + c * 1024 * 2, n, BF16),
                mean=lambda s_, n: sview(o_mean + s_ * 4, n, F32),
                vr=lambda s_, n: sview(o_vr + s_ * 4, n, F32))
            return d

        def ln_front_a(L, u):
            us, un, tl = UNITS[u]
            sq = L["sq"]
            for c in range(KC):
                if c % 2 == 0:
                    P.op("dve", lambda v, c=c: v.tensor_copy(out=xba(c, us, un), in_=xfa(c, us, un)),
                         reads=[B_xf[c][u]], writes=[B_xb[c][u]])
                else:
                    P.op("act", lambda a, c=c: a.activation(out=xba(c, us, un), in_=xfa(c, us, un), func=AF.Copy),
                         reads=[B_xf[c][u]], writes=[B_xb[c][u]])
            for c in range(KC):
                P.op("act", lambda a, c=c: a.activation(out=sq(c, un), in_=xfa(c, us, un), func=AF.Square),
                     reads=[B_xf[c][u]], writes=[L["B_sq"][c]])

        def ln_front_pe(L, u):
            us, un, tl = UNITS[u]
            sq = L["sq"]
            sm, sq_slot = next_unit_slot(), next_unit_slot()
            for which, slot in ((0, sm), (1, sq_slot)):
                for bi, ti in enumerate(tl):
                    ts, tn = TILES[ti]
                    for c in range(KC):
                        rhs = xba(c, ts, tn) if which == 0 else sq(c, un)[:, ts - us:ts - us + tn]
                        rb = B_xb[c][u] if which == 0 else L["B_sq"][c]
                        P.op("pe", lambda t, slot=slot, bi=bi, tn=tn, rhs=rhs, c=c: t.matmul(
                            psa(2 * slot + bi, 0, tn), ones[:], rhs, start=(c == 0), stop=(c == KC - 1)),
                            reads=[rb, B_const], writes=[B_bank[2 * slot + bi]], inc=(c == KC - 1))
            return sm, sq_slot

        def ln_front_chain(L, u, slots):
            us, un, tl = UNITS[u]
            mean, vr = L["mean"], L["vr"]
            sm, sq_slot = slots
            bm = [B_bank[2 * sm + bi] for bi in range(len(tl))]
            bq = [B_bank[2 * sq_slot + bi] for bi in range(len(tl))]
            pm = ps[:, sm * 1024:sm * 1024 + un]
            pq = ps[:, sq_slot * 1024:sq_slot * 1024 + un]
            P.op("act", lambda a: a.activation(out=mean(us, un), in_=pm, func=AF.Copy), reads=bm, writes=[L["B_mean"][u]])
            P.op("act", lambda a: a.activation(out=vr(us, un), in_=pm, func=AF.Square), reads=bm, writes=[L["B_vr"][u]])
            P.op("dve", lambda v: v.scalar_tensor_tensor(out=vr(us, un), in0=pq, scalar=LN_EPS, in1=vr(us, un),
                                                         op0=ALU.add, op1=ALU.subtract),
                 reads=bq + [L["B_vr"][u]], writes=[L["B_vr"][u]])
            P.op("act", lambda a: a.activation(out=vr(us, un), in_=vr(us, un), func=AF.Ln),
                 reads=[L["B_vr"][u]], writes=[L["B_vr"][u]])
            P.op("act", lambda a: a.activation(out=vr(us, un), in_=vr(us, un), func=AF.Exp, scale=-0.5),
                 reads=[L["B_vr"][u]], writes=[L["B_vr"][u]])

        def ln_front(L, u):
            ln_front_a(L, u)
            ln_front_chain(L, u, ln_front_pe(L, u))

        def ln_back(L, l, u, goff, boff, make_xb=True):
            us, un, tl = UNITS[u]
            mean, vr = L["mean"], L["vr"]
            for c in range(KC):
                eng = "pool" if (c >= 6 and un >= 512) else "dve"
                P.op(eng, lambda v, c=c: v.tensor_tensor(
                    out=xfa(c, us, un), in0=xfa(c, us, un), in1=mean(us, un), op=ALU.subtract),
                    reads=[B_xf[c][u], L["B_mean"][u]], writes=[B_xf[c][u]])
                P.op(eng, lambda v, c=c: v.tensor_tensor(
                    out=xfa(c, us, un), in0=xfa(c, us, un), in1=vr(us, un), op=ALU.mult),
                    reads=[B_xf[c][u], L["B_vr"][u]], writes=[B_xf[c][u]])
                if make_xb and c in (0, 2, 4):
                    P.op("dve", lambda v, c=c: v.tensor_scalar(
                        out=xba(c, us, un), in0=xfa(c, us, un), scalar1=para(l, goff + c), scalar2=para(l, boff + c),
                        op0=ALU.mult, op1=ALU.add),
                        reads=[B_xf[c][u], B_par], writes=[B_xb[c][u]])
                elif make_xb:
                    P.op("act", lambda a, c=c: a.activation(
                        out=xba(c, us, un), in_=xfa(c, us, un), func=AF.Identity, bias=para(l, boff + c),
                        scale=para(l, goff + c)),
                        reads=[B_xf[c][u], B_par], writes=[B_xb[c][u]])
                P.op("act", lambda a, c=c: a.activation(
                    out=xfa(c, us, un), in_=xfa(c, us, un), func=AF.Identity, bias=para(l, boff + c),
                    scale=para(l, goff + c)),
                    reads=[B_xf[c][u], B_par], writes=[B_xf[c][u]])

        def layer_norm(l, goff, boff, make_xb=True, L=None, fronts_done=()):
            if L is None:
                P.gc()
                L = ln_bufs()
            todo = [u for u in range(3) if u not in fronts_done]
            pend = None
            for u in todo:
                ln_front_a(L, u)
                if pend is not None:
                    ln_front_chain(L, *pend)
                pend = (u, ln_front_pe(L, u))
            if pend is not None:
                ln_front_chain(L, *pend)
            for u in range(3):
                ln_back(L, l, u, goff, boff, make_xb)

        def proj_unit(slot_w, col0, u, slot_p, rhs_fn, rhs_bufs, nk=KC, kfn=None):
            us, un, tl = UNITS[u]
            for bi, ti in enumerate(tl):
                ts, tn = TILES[ti]
                for k in range(nk):
                    kk = k if kfn is None else kfn(k)
                    P.op("pe", lambda t, bi=bi, tn=tn, ts=ts, k=k, kk=kk: t.matmul(
                        psa(2 * slot_p + bi, 0, tn), wra(slot_w, kk, col0, 128), rhs_fn(k, ts, tn),
                        start=(k == 0), stop=(k == nk - 1)),
                        reads=[B_wr[slot_w], rhs_bufs(k, u, ti)], writes=[B_bank[2 * slot_p + bi]],
                        inc=(k == nk - 1))
            return [B_bank[2 * slot_p + bi] for bi in range(len(tl))]

        def ffn(l):
            P.gc()
            ar = Arena()
            o_a = ar.take(NA * T * 2)
            o_wd = ar.take(6 * D_MODEL * 2)
            o_h = [ar.take(2 * 1026 * 4) for _ in range(2)]
            o_c = [ar.take(2 * 1024 * 4) for _ in range(3)]
            B_a = [[sbuf_(f"a{s}_{u}", o_a + (s * T + UNITS[u][0]) * 2, UNITS[u][1], BF16) for u in range(3)] for s in range(NA)]
            B_wd = sbuf_("wd", o_wd, 6 * D_MODEL, BF16)
            B_h = [[sbuf_(f"h{q}_{gv}", o_h[q] + gv * 1026 * 4, 1026, F32) for gv in range(2)] for q in range(2)]
            B_c = [[sbuf_(f"c{q}_{gv}", o_c[q] + gv * 1024 * 4, 1024, F32) for gv in range(2)] for q in range(3)]
            aview = lambda s, c0, n: sview(o_a + (s * T + c0) * 2, n, BF16)
            wdv = lambda jj, c0, n: sview(o_wd + (jj * D_MODEL + c0) * 2, n, BF16)
            hv = lambda q, gv, c0, n: sview(o_h[q] + (gv * 1026 + c0) * 4, n, F32)
            cv = lambda q, gv, c0, n: sview(o_c[q] + (gv * 1024 + c0) * 4, n, F32)

            def load_wd(s):
                P.dma("pool", "wd", lambda g, s=s: g.dma_start(out=sview(o_wd, 6 * D_MODEL, BF16), in_=wdown_d[4 * l + s]),
                      writes=[B_wd])

            jobs = []
            st = dict(j=0, prev=None)

            def u_front(m, slot_w, u):
                us, un, tl = UNITS[u]
                q = st["j"] % 2
                qc = st["j"] % 3
                st["j"] += 1
                sg, sv_ = next_unit_slot(), next_unit_slot()
                bg = proj_unit(slot_w, 0, u, sg, lambda k, ts, tn: xba(k, ts, tn), lambda k, u, ti: B_xb[k][u])
                bv = proj_unit(slot_w, 128, u, sv_, lambda k, ts, tn: xba(k, ts, tn), lambda k, u, ti: B_xb[k][u])
                pg = ps[:, sg * 1024:sg * 1024 + un]
                pv = ps[:, sv_ * 1024:sv_ * 1024 + un]
                if u == 0:
                    for gv in range(2):
                        P.op("dve", lambda v, q=q, gv=gv: v.memset(hv(q, gv, 0, 2), 0.0), writes=[B_h[q][gv]])
                else:
                    pun = UNITS[u - 1][1]
                    for gv in range(2):
                        P.op("act", lambda a, q=q, gv=gv, pun=pun: a.activation(
                            out=hv(q, gv, 0, 2), in_=hv(q ^ 1, gv, pun, 2), func=AF.Copy),
                            reads=[B_h[q ^ 1][gv]], writes=[B_h[q][gv]])
                P.op("act", lambda a, q=q, pg=pg, un=un, m=m: a.activation(
                    out=hv(q, 0, 2, un), in_=pg, func=AF.Identity, bias=para(l, PL_BUP + m), scale=1.0),
                    reads=bg + [B_par], writes=[B_h[q][0]])
                P.op("act", lambda a, q=q, pv=pv, un=un, m=m: a.activation(
                    out=hv(q, 1, 2, un), in_=pv, func=AF.Identity, bias=para(l, PL_BUP + NFF + m), scale=1.0),
                    reads=bv + [B_par], writes=[B_h[q][1]])
                for gv in range(2):
                    ch = m + gv * NFF
                    P.op("act", lambda a, q=q, qc=qc, gv=gv, un=un, ch=ch: a.activation(
                        out=cv(qc, gv, 0, un), in_=hv(q, gv, 0, un), func=AF.Identity,
                        bias=para(l, PL_CB + ch), scale=para(l, PL_CW + ch)),
                        reads=[B_h[q][gv], B_par], writes=[B_c[qc][gv]])
                for gv in range(2):
                    ch = m + gv * NFF
                    for k in (1, 2):
                        P.op("dve", lambda v, q=q, qc=qc, gv=gv, un=un, ch=ch, k=k: v.scalar_tensor_tensor(
                            out=cv(qc, gv, 0, un), in0=hv(q, gv, k, un), scalar=para(l, PL_CW + 44 * k + ch),
                            in1=cv(qc, gv, 0, un), op0=ALU.mult, op1=ALU.add),
                            reads=[B_h[q][gv], B_c[qc][gv], B_par], writes=[B_c[qc][gv]])
                return (m, u, qc)

            def u_back(job):
                m, u, q = job
                us, un, tl = UNITS[u]
                aslot = m % NA
                P.op("act", lambda a, q=q, un=un: a.activation(out=cv(q, 0, 0, un), in_=cv(q, 0, 0, un), func=AF.Gelu_apprx_tanh),
                     reads=[B_c[q][0]], writes=[B_c[q][0]])
                P.op("dve", lambda v, q=q, un=un, us=us, aslot=aslot: v.tensor_tensor(
                    out=aview(aslot, us, un), in0=cv(q, 0, 0, un), in1=cv(q, 1, 0, un), op=ALU.mult),
                    reads=[B_c[q][0], B_c[q][1]], writes=[B_a[aslot][u]])

            def u_pair(m):
                slot_w = w_next()
                for u in range(3):
                    job = u_front(m, slot_w, u)
                    if st["prev"] is not None:
                        u_back(st["prev"])
                    st["prev"] = job

            def flush():
                if st["prev"] is not None:
                    u_back(st["prev"])
                    st["prev"] = None

            def d_slab(s, L=None):
                j0, nj = SLABS[s]
                for u, (us, un, tl) in enumerate(UNITS):
                    if L is not None and u > 0:
                        ln_front(L, u - 1)
                    for o in range(KC):
                        slot = next_unit_slot()
                        for bi, ti in enumerate(tl):
                            ts, tn = TILES[ti]
                            for jj in range(nj):
                                aslot = (j0 + jj) % NA
                                P.op("pe", lambda t, slot=slot, bi=bi, tn=tn, ts=ts, jj=jj, o=o, aslot=aslot: t.matmul(
                                    psa(2 * slot + bi, 0, tn), wdv(jj, o * 128, 128), aview(aslot, ts, tn),
                                    start=(jj == 0), stop=(jj == nj - 1)),
                                    reads=[B_wd, B_a[aslot][u]], writes=[B_bank[2 * slot + bi]], inc=(jj == nj - 1))
                        bb = [B_bank[2 * slot + bi] for bi in range(len(tl))]
                        pp = ps[:, slot * 1024:slot * 1024 + un]
                        if s == 0:
                            P.op("dve", lambda v, o=o, us=us, un=un, pp=pp: v.scalar_tensor_tensor(
                                out=xfa(o, us, un), in0=xfa(o, us, un), scalar=ALPHA, in1=pp, op0=ALU.mult, op1=ALU.add),
                                reads=bb + [B_xf[o][u]], writes=[B_xf[o][u]])
                        else:
                            P.op("dve", lambda v, o=o, us=us, un=un, pp=pp: v.tensor_tensor(
                                out=xfa(o, us, un), in0=pp, in1=xfa(o, us, un), op=ALU.add),
                                reads=bb + [B_xf[o][u]], writes=[B_xf[o][u]])

            load_wd(0)
            for s, (j0, nj) in enumerate(SLABS):
                for jj in range(nj):
                    u_pair(j0 + jj)
                    if s > 0 and jj == 1:
                        d_slab(s - 1)
                        load_wd(s)
            flush()
            L = ln_bufs()
            d_slab(len(SLABS) - 1, L)
            ln_front(L, 2)
            return L

        def mixer_even(l):
            i = l // 2
            P.gc()
            ar = Arena()
            o_cat = ar.take(KC * T * 2)
            o_vln = ar.take(NBLK * 512 * 2)
            o_wm = ar.take(4 * 128 * 2)
            o_bs = ar.take(2048 * 4)
            o_lg = ar.take(512 * 4)
            o_lb = ar.take(512 * 4)
            o_st = ar.take(NBLK * 4 * 6 * 4)
            o_mv = ar.take(NBLK * 4 * 2 * 4)
            o_sd = ar.take(NBLK * 4 * 4)
            o_tmp = [ar.take(514 * 4) for _ in range(6)]
            B_cat = [[sbuf_(f"cat{c}_{t}", o_cat + (c * T + TILES[t][0]) * 2, TILES[t][1], BF16) for t in range(5)] for c in range(KC)]
            B_vg = [sbuf_(f"vg{b}", o_cat + b * 512 * 4, 512, F32) for b in range(NBLK)]
            B_vln = [sbuf_(f"vln{b}", o_vln + b * 512 * 2, 512, BF16) for b in range(NBLK)]
            B_gm = sbuf_("gmw", o_wm, (o_lb + 2048 - o_wm) // 2, BF16)
            B_st = sbuf_("st", o_st, (o_sd + NBLK * 16 - o_st) // 4, F32)
            B_tmp = [sbuf_(f"tmp{k}", o_tmp[k], 514, F32) for k in range(6)]
            cat = lambda c, s, n: sview(o_cat + (c * T + s) * 2, n, BF16)
            vg = lambda b, c0, n: sview(o_cat + (b * 512 + c0) * 4, n, F32)
            vln = lambda b, c0, n: sview(o_vln + (b * 512 + c0) * 2, n, BF16)
            wm = lambda h: sview(o_wm + h * 128 * 2, 128, BF16)
            bsv = lambda h, n: sview(o_bs + h * 512 * 4, n, F32)
            tmp = lambda k, c0, n: sview(o_tmp[k] + c0 * 4, n, F32)
            stv = lambda b, h: sview(o_st + (b * 4 + h) * 6 * 4, 6, F32)
            mvv = lambda b, h, j: sview(o_mv + ((b * 4 + h) * 2 + j) * 4, 1, F32)

            P.op("dve", lambda v: v.memset(sview(o_wm, 512, BF16), 0.0), writes=[B_gm])
            wmt = sview(o_wm, 512, BF16)
            P.dma("pool", "gm", lambda g: g.dma_start(out=wmt[0:64, :], in_=gws_d[i, 0:64].rearrange("p h i -> p (h i)")),
                  writes=[B_gm])
            for h in range(4):
                P.dma("pool", "gm", lambda g, h=h: g.dma_start(out=wmt[64:128, h * 128 + 64:h * 128 + 128],
                                                                 in_=gws_d[i, 64:128, h, 64:128]), writes=[B_gm])
            P.dma("sp", "gm", lambda s: s.dma_start(out=sview(o_bs, 2048, F32), in_=gbs_d[i]), writes=[B_gm])
            P.dma("sp", "gm", lambda s: s.dma_start(out=sview(o_lg, 512, F32), in_=glg_d[i]), writes=[B_gm])
            P.dma("sp", "gm", lambda s: s.dma_start(out=sview(o_lb, 512, F32), in_=glb_d[i]), writes=[B_gm])
            gt = P.dma_ch["gm"]
            B_gm.w = Tick("dma:gm", gt[1], gt[0], gt[1])

            sv0, sv1 = w_next(1), w_next(1)
            for b in range(NBLK):
                u = unit_of_tile(b // 4)
                bank = next_bank()
                for it, slot_w in enumerate((sv0, sv1)):
                    for k in range(KC):
                        P.op("pe", lambda t, bank=bank, it=it, slot_w=slot_w, k=k, b=b: t.matmul(
                            psa(bank, it * 256, 256), xba(k, b * 128, 128), wra(slot_w, k, 0, 256),
                            start=(k == 0), stop=(k == KC - 1)),
                            reads=[B_wr[slot_w], B_xb[k][u]], writes=[B_bank[bank]], inc=(k == KC - 1))
                P.op("act", lambda a, bank=bank, b=b: a.activation(out=vg(b, 0, 512), in_=psa(bank, 0, 512), func=AF.Gelu_apprx_tanh),
                     reads=[B_bank[bank]], writes=[B_vg[b]])
                for h in range(4):
                    P.op("dve", lambda v, b=b, h=h: v.bn_stats(out=stv(b, h), in_=vg(b, h * 128, 128)),
                         reads=[B_vg[b]], writes=[B_st])
                for h in range(4):
                    P.op("dve", lambda v, b=b, h=h: v.bn_aggr(out=sview(o_mv + (b * 4 + h) * 8, 2, F32), in_=stv(b, h)),
                         reads=[B_st], writes=[B_st])
            var_all = scr[:, o_mv // 4 + 1:o_mv // 4 + 1 + 2 * NBLK * 4:2]
            P.op("act", lambda a: a.activation(out=sview(o_sd, NBLK * 4, F32), in_=var_all, func=AF.Ln, bias=epsb[:, 0:1], scale=1.0),
                 reads=[B_st, B_const], writes=[B_st])
            P.op("act", lambda a: a.activation(out=sview(o_sd, NBLK * 4, F32), in_=sview(o_sd, NBLK * 4, F32), func=AF.Exp, scale=-0.5),
                 reads=[B_st], writes=[B_st])
            for b in range(NBLK):
                for h in range(4):
                    P.op("dve", lambda v, b=b, h=h: v.tensor_scalar(
                        out=vg(b, h * 128, 128), in0=vg(b, h * 128, 128), scalar1=mvv(b, h, 0),
                        scalar2=sview(o_sd + (b * 4 + h) * 4, 1, F32), op0=ALU.subtract, op1=ALU.mult),
                        reads=[B_vg[b], B_st], writes=[B_vg[b]])
                P.op("dve", lambda v, b=b: v.tensor_tensor(out=vg(b, 0, 512), in0=vg(b, 0, 512), in1=sview(o_lg, 512, F32), op=ALU.mult),
                     reads=[B_vg[b], B_gm], writes=[B_vg[b]])
                P.op("dve", lambda v, b=b: v.tensor_tensor(out=vln(b, 0, 512), in0=vg(b, 0, 512), in1=sview(o_lb, 512, F32), op=ALU.add),
                     reads=[B_vg[b], B_gm], writes=[B_vln[b]])

            if debug == "vln" and l == 0:
                P.dma("sp", "out", lambda s_: s_.dma_start(out=dbgb[:, 0:NBLK * 512], in_=sview(o_vln, NBLK * 512, BF16)),
                      reads=B_vln)
            tq = dict(q=0)
            for hp in range(2):
                slot_w = w_next()
                for hh in range(2):
                    h = 2 * hp + hh
                    for ti, (ts, tn) in enumerate(TILES):
                        u = unit_of_tile(ti)
                        q = tq["q"]
                        tq["q"] ^= 1
                        bu, bs_ = next_bank(), next_bank()
                        for k in range(KC):
                            P.op("pe", lambda t, bu=bu, k=k, ts=ts, tn=tn, hh=hh, slot_w=slot_w: t.matmul(
                                psa(bu, 0, tn), wra(slot_w, k, hh * 128, 128), xba(k, ts, tn),
                                start=(k == 0), stop=(k == KC - 1)),
                                reads=[B_wr[slot_w], B_xb[k][u]], writes=[B_bank[bu]], inc=(k == KC - 1))
                        for bi in range(tn // 128):
                            b = ts // 128 + bi
                            P.op("pe", lambda t, bs_=bs_, bi=bi, b=b, h=h: t.matmul(
                                psa(bs_, bi * 128, 128), vln(b, h * 128, 128), wm(h), start=True, stop=True),
                                reads=[B_vln[b], B_gm], writes=[B_bank[bs_]], inc=(bi == tn // 128 - 1))
                        P.op("act", lambda a, q=q, bu=bu, tn=tn: a.activation(out=tmp(q, 0, tn), in_=psa(bu, 0, tn), func=AF.Gelu_apprx_tanh),
                             reads=[B_bank[bu]], writes=[B_tmp[q]])
                        P.op("dve", lambda v, q=q, bs_=bs_, tn=tn, h=h: v.tensor_tensor(
                            out=tmp(2 + q, 0, tn), in0=psa(bs_, 0, tn), in1=bsv(h, tn), op=ALU.add),
                            reads=[B_bank[bs_], B_gm], writes=[B_tmp[2 + q]])
                        P.op("dve", lambda v, q=q, tn=tn, ts=ts, h=h: v.tensor_tensor(
                            out=cat(h, ts, tn), in0=tmp(2 + q, 0, tn), in1=tmp(q, 0, tn), op=ALU.mult),
                            reads=[B_tmp[q], B_tmp[2 + q]], writes=[B_cat[h][ti]])

            cur = dict(slot=None, pos=2)

            def next_chunk():
                if cur["pos"] == 2:
                    cur["slot"] = w_next(1)
                    cur["pos"] = 0
                r = (cur["slot"], cur["pos"] * 128)
                cur["pos"] += 1
                return r

            sq_ = dict(q=0)
            for j in range(4):
                wh, wgc, wgb = next_chunk(), next_chunk(), next_chunk()
                for ti, (ts, tn) in enumerate(TILES):
                    u = unit_of_tile(ti)
                    q = sq_["q"]
                    sq_["q"] ^= 1
                    banks = []
                    for (slot_w, c0) in (wh, wgc, wgb):
                        bk = next_bank()
                        banks.append(bk)
                        for k in range(KC):
                            P.op("pe", lambda t, bk=bk, k=k, ts=ts, tn=tn, slot_w=slot_w, c0=c0: t.matmul(
                                psa(bk, 0, tn), wra(slot_w, k, c0, 128), xba(k, ts, tn),
                                start=(k == 0), stop=(k == KC - 1)),
                                reads=[B_wr[slot_w], B_xb[k][u]], writes=[B_bank[bk]], inc=(k == KC - 1))
                    bh, bgc, bgb = banks
                    P.op("act", lambda a, q=q, bh=bh, tn=tn: a.activation(out=tmp(q, 0, tn), in_=psa(bh, 0, tn), func=AF.Copy),
                         reads=[B_bank[bh]], writes=[B_tmp[q]])
                    if ti == 0:
                        P.op("dve", lambda v, q=q: v.memset(tmp(2 + q, 0, 2), 0.0), writes=[B_tmp[2 + q]])
                    else:
                        ptn = TILES[ti - 1][1]
                        P.op("act", lambda a, q=q, ptn=ptn: a.activation(out=tmp(2 + q, 0, 2), in_=tmp(2 + (q ^ 1), ptn, 2), func=AF.Copy),
                             reads=[B_tmp[2 + (q ^ 1)]], writes=[B_tmp[2 + q]])
                    P.op("dve", lambda v, q=q, bgc=bgc, tn=tn: v.tensor_tensor(
                        out=tmp(2 + q, 2, tn), in0=psa(bgc, 0, tn), in1=tmp(q, 0, tn), op=ALU.mult),
                        reads=[B_bank[bgc], B_tmp[q]], writes=[B_tmp[2 + q]])
                    P.op("dve", lambda v, q=q, tn=tn, j=j: v.tensor_scalar(
                        out=tmp(4 + q, 0, tn), in0=tmp(2 + q, 0, tn), scalar1=para(l, PL_SW + j), scalar2=None, op0=ALU.mult),
                        reads=[B_tmp[2 + q], B_par], writes=[B_tmp[4 + q]])
                    for k in (1, 2):
                        P.op("dve", lambda v, q=q, tn=tn, j=j, k=k: v.scalar_tensor_tensor(
                            out=tmp(4 + q, 0, tn), in0=tmp(2 + q, k, tn), scalar=para(l, PL_SW + 4 * k + j),
                            in1=tmp(4 + q, 0, tn), op0=ALU.mult, op1=ALU.add),
                            reads=[B_tmp[2 + q], B_tmp[4 + q], B_par], writes=[B_tmp[4 + q]])
                    P.op("dve", lambda v, q=q, bgb=bgb, tn=tn, ts=ts, j=j: v.tensor_tensor(
                        out=cat(4 + j, ts, tn), in0=psa(bgb, 0, tn), in1=tmp(4 + q, 0, tn), op=ALU.mult),
                        reads=[B_bank[bgb], B_tmp[4 + q]], writes=[B_cat[4 + j][ti]])

            if debug == "cat" and l == 0:
                P.dma("sp", "out", lambda s_: s_.dma_start(out=dbgb, in_=sview(o_cat, KC * T, BF16)),
                      reads=[b_ for row in B_cat for b_ in row])
            for it in range(4):
                slot_w = w_next()
                for mo in range(2):
                    o = 2 * it + mo
                    for u, (us, un, tl) in enumerate(UNITS):
                        slot = next_unit_slot()
                        bb = proj_unit(slot_w, mo * 128, u, slot, lambda k, ts, tn: cat(k, ts, tn),
                                       lambda k, u, ti: B_cat[k][ti])
                        pp = ps[:, slot * 1024:slot * 1024 + un]
                        P.op("dve", lambda v, o=o, us=us, un=un, pp=pp: v.scalar_tensor_tensor(
                            out=xfa(o, us, un), in0=xfa(o, us, un), scalar=ALPHA, in1=pp, op0=ALU.mult, op1=ALU.add),
                            reads=bb + [B_xf[o][u]], writes=[B_xf[o][u]])

        def mixer_odd(l):
            i = l // 2
            P.gc()
            ar = Arena()
            SP = 16
            o_p = ar.take(KC * T * 2)
            o_s = [ar.take((T + SP) * 4) for _ in range(6)]
            o_t = [ar.take(16 * 4) for _ in range(2)]
            B_p = [[sbuf_(f"p{c}_{u}", o_p + (c * T + UNITS[u][0]) * 2, UNITS[u][1], BF16) for u in range(3)] for c in range(KC)]
            B_s = [sbuf_(f"S{k}", o_s[k], T + SP, F32) for k in range(6)]
            B_t = [sbuf_(f"ptmp{k}", o_t[k], 16, F32) for k in range(2)]
            pv = lambda c, s_, n: sview(o_p + (c * T + s_) * 2, n, BF16)
            sv = lambda k, s_, n: sview(o_s[k] + (SP + s_) * 4, n, F32)
            slot_w = w_next()
            for k in range(6):
                P.op("dve", lambda v, k=k: v.memset(sview(o_s[k], SP, F32), 0.0), writes=[B_s[k]])
            L = ln_bufs()
            def sums(c):
                g = c // 2
                w = C_WINDOWS[g]
                seng = "pool" if g == 2 else "dve"
                sb_ = (2 + 2 * (c - 4)) if g == 2 else 0
                P.op(seng, lambda v, c=c, sb_=sb_: v.tensor_tensor(
                    out=sv(sb_, 1, T - 1), in0=xfa(c, 1, T - 1), in1=xfa(c, 0, T - 1), op=ALU.add),
                    reads=list(B_xf[c]), writes=[B_s[sb_]])
                P.op("act", lambda a, c=c, sb_=sb_: a.activation(out=sv(sb_, 0, 1), in_=xfa(c, 0, 1), func=AF.Copy),
                     reads=[B_xf[c][0]], writes=[B_s[sb_]])
                src, sh = sb_, 2
                while sh < w:
                    dst = sb_ + (1 - (src - sb_))
                    P.op(seng, lambda v, src=src, dst=dst, sh=sh: v.tensor_tensor(
                        out=sv(dst, 0, T), in0=sv(src, 0, T), in1=sv(src, -sh, T), op=ALU.add),
                        reads=[B_s[src]], writes=[B_s[dst]])
                    src = dst
                    sh *= 2
                return src

            def pfin(c, src):
                g = c // 2
                w = C_WINDOWS[g]
                tb = 1 if g == 2 else 0
                P.op("dve", lambda v, c=c, src=src, w=w: v.scalar_tensor_tensor(
                    out=pv(c, 0, T), in0=sv(src, 0, T), scalar=1.0 / w, in1=xfa(c, 0, T), op0=ALU.mult, op1=ALU.subtract),
                    reads=[B_s[src]] + list(B_xf[c]), writes=list(B_p[c]))
                P.op("dve", lambda v, src=src, w=w, tb=tb: v.tensor_tensor(
                    out=sview(o_t[tb], w - 1, F32), in0=sv(src, 0, w - 1), in1=invc[:, 0:w - 1], op=ALU.mult),
                    reads=[B_s[src], B_const], writes=[B_t[tb]])
                P.op("dve", lambda v, c=c, w=w, tb=tb: v.tensor_tensor(
                    out=pv(c, 0, w - 1), in0=sview(o_t[tb], w - 1, F32), in1=xfa(c, 0, w - 1), op=ALU.subtract),
                    reads=[B_t[tb], B_xf[c][0]], writes=[B_p[c][0]])

            psrc = {c: sums(c) for c in (4, 5)}
            for c in (0, 1, 2, 3, 6, 7):
                pfin(c, sums(c))
            for c in (4, 5):
                pfin(c, psrc[c])
            for u, (us, un, tl) in enumerate(UNITS):
                for o in range(KC):
                    g, mo = o // 2, o % 2
                    slot = next_unit_slot()
                    bb = proj_unit(slot_w, mo * 128, u, slot, lambda k, ts, tn, g=g: pv(2 * g + k, ts, tn),
                                   lambda k, u, ti, g=g: B_p[2 * g + k][u], nk=2, kfn=lambda k, g=g: 2 * g + k)
                    pp = ps[:, slot * 1024:slot * 1024 + un]
                    P.op("act", lambda a, o=o, us=us, un=un: a.activation(
                        out=xfa(o, us, un), in_=xfa(o, us, un), func=AF.Copy, scale=ALPHA),
                        reads=[B_xf[o][u]], writes=[B_xf[o][u]])
                    P.op("dve", lambda v, o=o, us=us, un=un, pp=pp: v.scalar_tensor_tensor(
                        out=xfa(o, us, un), in0=pp, scalar=para(l, PL_PS + o), in1=xfa(o, us, un), op0=ALU.mult, op1=ALU.add),
                        reads=bb + [B_xf[o][u], B_par], writes=[B_xf[o][u]])
                ln_front(L, u)
            return L

        for l in range(nlayers):
            last = (l == nlayers - 1)
            if l % 2 == 0:
                mixer_even(l)
                Lm, fd = None, ()
            else:
                Lm, fd = mixer_odd(l), (0, 1, 2)
            layer_norm(l, PL_LNMG, PL_LNMB, make_xb=not (last and stop_after_mixer), L=Lm, fronts_done=fd)
            if last and stop_after_mixer:
                break
            L = ffn(l)
            layer_norm(l, PL_LNFG, PL_LNFB, make_xb=(not last) and ((l + 1) % 2 == 0), L=L, fronts_done=(0, 1, 2))

        for u, (us, un, tl) in enumerate(UNITS):
            for c in range(KC):
                P.dma("sp", "out", lambda s, c=c, us=us, un=un: s.dma_start(
                    out=outT[c * 128:(c + 1) * 128, us:us + un], in_=xfa(c, us, un)), reads=[B_xf[c][u]])
        ot = P.dma_ch["out"]
        P.final_wait("sp", [Tick("dma:out", ot[1], ot[0], ot[1])])

        with nc.Block() as block:
            @block.tensor
            def _(h):
                P.replay("pe", h)

            @block.scalar
            def _(h):
                P.replay("act", h)

            @block.vector
            def _(h):
                P.replay("dve", h)

            @block.gpsimd
            def _(h):
                P.replay("pool", h)

            @block.sync
            def _(h):
                P.replay("sp", h)
    return nc


def core_slices():
    sl = []
    for b in range(BATCH):
        sl.append((b, 0, T))
        sl.append((b, SEQ - T, SEQ))
    return sl


def kernel(**inputs):
    x = np.asarray(inputs["x"], np.float32)
    w = prep_weights(inputs)
    in_maps = []
    for (b, s, e) in core_slices():
        m = dict(w)
        m["xT"] = np.ascontiguousarray(x[b, s:e, :].T)
        in_maps.append(m)
    nc = build_program()
    res = run_bass_kernel_spmd(nc, in_maps, core_ids=list(range(8)))
    out = np.empty((BATCH, SEQ, D_MODEL), np.float32)
    for ci, (b, s, e) in enumerate(core_slices()):
        oT = res.results[ci]["outT"]
        if s == 0:
            out[b, 0:OWN_A, :] = oT[:, 0:OWN_A].T
        else:
            out[b, OWN_A:SEQ, :] = oT[:, OWN_A - s:T].T
    return out
```
